# Optimizing a Trainium2 kernel written in Bass

```python
import jax, jax.numpy as jnp
from jax import lax
import numpy as np

D_MODEL = 1024
BATCH = 8
SEQ = 4096
DEPTH = 4
DEC_BATCH = 16
DEC_SEQ = 32
PAST_LEN = 4096

CHUNK = 64
N_MIXERS = 3
N_A = (DEPTH + 2) // 3
N_B = (DEPTH + 1) // 3
N_C = DEPTH // 3
D_FF = 2816
EPS = 1e-6
RET_H = 8
RET_DK = 128
RET_DV = 256
RET_QK = RET_H * RET_DK
RET_V = RET_H * RET_DV
RET_IN = 2 * RET_QK + 2 * RET_V
ROPE_BASE = 10000.0
WINDOW = 128
WIN_CHUNKS = WINDOW // CHUNK
SWA_HQ = 16
SWA_HKV = 2
SWA_G = SWA_HQ // SWA_HKV
SWA_DH = 64
SWA_Q = SWA_HQ * SWA_DH
SWA_KV = SWA_HKV * SWA_DH
SWA_IN = SWA_Q + 2 * SWA_KV
HG_H = 8
HG_DK = 128
HG_DV = 128
HG_QK = HG_H * HG_DK
HG_V = HG_H * HG_DV
HG_IN = 2 * HG_QK + 2 * HG_V

kernel_name = "hybrid_chunk_causal_retention_swa_hgrn2_step"


def rmsnorm(x, g):
    xf = x.astype(jnp.float32)
    y = xf * lax.rsqrt(jnp.mean(xf * xf, axis=-1, keepdims=True) + EPS)
    return (y * g.astype(jnp.float32)).astype(x.dtype)


def head_rmsnorm(o, g):
    o = o * lax.rsqrt(jnp.mean(o * o, axis=-1, keepdims=True) + EPS)
    return o.reshape(o.shape[0], o.shape[1], -1) * g.astype(jnp.float32)


def swiglu(h, w_in, w_out):
    a, u = jnp.split(h @ w_in, 2, axis=-1)
    return (jax.nn.silu(a) * u) @ w_out


def rope(x, pos):
    half = x.shape[-1] // 2
    inv = ROPE_BASE ** (-jnp.arange(half, dtype=jnp.float32) / half)
    ang = pos[:, None] * inv[None, :]
    cos = jnp.cos(ang)[None, :, None, :]
    sin = jnp.sin(ang)[None, :, None, :]
    x1, x2 = x[..., :half], x[..., half:]
    return jnp.concatenate([x1 * cos - x2 * sin, x1 * sin + x2 * cos], axis=-1)


def chunk_scan(step, S0, seqs, chunk):
    b, L = seqs[0].shape[:2]
    n = L // chunk
    xs = tuple(jnp.moveaxis(a.reshape(b, n, chunk, *a.shape[2:]), 1, 0) for a in seqs)
    S, out = lax.scan(lambda S, xc: step(S, *xc), S0, xs)
    out = jnp.moveaxis(out, 0, 1)
    return S, out.reshape(b, L, *out.shape[3:])


def retention_log_decay():
    return jnp.log1p(-(2.0 ** (-5.0 - jnp.arange(RET_H, dtype=jnp.float32))))


def retention_chunk(S, q, k, v, log_gamma):
    L = q.shape[1]
    pos = jnp.arange(L, dtype=jnp.float32)
    dist = jnp.abs(pos[:, None] - pos[None, :])
    decay = jnp.exp(log_gamma[:, None, None] * dist)
    scores = jnp.einsum('bihd,bjhd->bhij', q, k) * decay[None]
    inner = jnp.einsum('bhij,bjhe->bihe', scores, v)
    cross_dec = jnp.exp(log_gamma[None, :] * (pos[:, None] + 1.0))
    cross = jnp.einsum('bihd,bhde->bihe', q, S) * cross_dec[None, :, :, None]
    k_dec = k * jnp.exp(log_gamma[None, :] * (L - 1.0 - pos[:, None]))[None, :, :, None]
    S_new = jnp.exp(log_gamma * L)[None, :, None, None] * S + jnp.einsum('bjhd,bjhe->bhde', k_dec, v)
    return S_new, inner + cross


def retention_mixer(h, S0, pos0, chunk, w_in, w_out, gn_g):
    b, L, _ = h.shape
    q, k, v, g = jnp.split(h @ w_in, [RET_QK, 2 * RET_QK, 2 * RET_QK + RET_V], axis=-1)
    pos = pos0 + jnp.arange(L, dtype=jnp.float32)
    q = rope(q.reshape(b, L, RET_H, RET_DK).astype(jnp.float32), pos)
    k = rope(k.reshape(b, L, RET_H, RET_DK).astype(jnp.float32), pos) * (RET_DK ** -0.5)
    v = v.reshape(b, L, RET_H, RET_DV).astype(jnp.float32)
    log_gamma = retention_log_decay()
    S, o = chunk_scan(lambda S, qc, kc, vc: retention_chunk(S, qc, kc, vc, log_gamma),
                      S0.astype(jnp.float32), (q, k, v), chunk)
    o = head_rmsnorm(o, gn_g)
    out = (jax.nn.silu(g.astype(jnp.float32)) * o) @ w_out
    return out.astype(h.dtype), S


def swa_project(h, w_in):
    b, L, _ = h.shape
    q, k, v = jnp.split(h @ w_in, [SWA_Q, SWA_Q + SWA_KV], axis=-1)
    return (q.reshape(b, L, SWA_HKV, SWA_G, SWA_DH), k.reshape(b, L, SWA_HKV, SWA_DH),
            v.reshape(b, L, SWA_HKV, SWA_DH))


def sink_attention(q, k, v, mask, sink):
    s = jnp.einsum('bnqhgd,bnkhd->bnhgqk', q, k).astype(jnp.float32) * (SWA_DH ** -0.5)
    s = jnp.where(mask[None, :, None, None], s, -jnp.inf)
    sk = sink.astype(jnp.float32).reshape(1, 1, SWA_HKV, SWA_G, 1, 1)
    m = jnp.maximum(s.max(axis=-1, keepdims=True), sk)
    p = jnp.exp(s - m)
    w = p / (p.sum(axis=-1, keepdims=True) + jnp.exp(sk - m))
    return jnp.einsum('bnhgqk,bnkhd->bnqhgd', w, v.astype(jnp.float32))


def swa_prompt(h, w_in, w_out, sink):
    b, L, _ = h.shape
    n = L // CHUNK
    q, k, v = swa_project(h, w_in)

    def band(a):
        ac = a.reshape(b, n, CHUNK, SWA_HKV, SWA_DH)
        ap = jnp.pad(ac, ((0, 0), (WIN_CHUNKS, 0), (0, 0), (0, 0), (0, 0)))
        return jnp.concatenate([ap[:, s:s + n] for s in range(WIN_CHUNKS + 1)], axis=2)

    slot_chunk = jnp.repeat(jnp.arange(WIN_CHUNKS + 1) - WIN_CHUNKS, CHUNK)
    key_chunk = jnp.arange(n)[:, None] + slot_chunk[None, :]
    mask = (key_chunk >= 0)[:, None, :]
    o = sink_attention(q.reshape(b, n, CHUNK, SWA_HKV, SWA_G, SWA_DH), band(k), band(v), mask, sink)
    out = o.reshape(b, L, SWA_Q) @ w_out
    return out.astype(h.dtype), k[:, L - WINDOW:], v[:, L - WINDOW:]


def swa_sample(h, k_cache, v_cache, w_in, w_out, sink):
    b, L, _ = h.shape
    q, k, v = swa_project(h, w_in)
    kk = jnp.concatenate([k_cache, k], axis=1)[:, None]
    vv = jnp.concatenate([v_cache, v], axis=1)[:, None]
    mask = jnp.ones((1, 1, k_cache.shape[1] + L), dtype=bool)
    o = sink_attention(q[:, None], kk, vv, mask, sink)
    out = o.reshape(b, L, SWA_Q) @ w_out
    return out.astype(h.dtype), k, v


def forget_lower_bounds(lb_param):
    c = jnp.cumsum(jax.nn.softmax(lb_param.astype(jnp.float32), axis=0), axis=0)
    return c - c[0]


def hgrn2_chunk(S, q, k, v, logf):
    L = q.shape[1]
    bcum = jnp.cumsum(logf, axis=1)
    tri = jnp.tril(jnp.ones((L, L), dtype=bool))
    diff = bcum[:, :, None] - bcum[:, None, :]
    decay = jnp.exp(jnp.where(tri[None, :, :, None, None], diff, -jnp.inf))
    scores = jnp.einsum('btshd,bshd->bhts', q[:, :, None] * decay, k)
    inner = jnp.einsum('bhts,bshe->bthe', scores, v)
    cross = jnp.einsum('bthd,bhde->bthe', q * jnp.exp(bcum), S)
    bL = bcum[:, -1]
    k_dec = k * jnp.exp(bL[:, None] - bcum)
    S_new = jnp.exp(bL)[..., None] * S + jnp.einsum('bshd,bshe->bhde', k_dec, v)
    return S_new, inner + cross


def hgrn2_mixer(h, S0, chunk, lb, w_in, w_out, gn_g):
    b, L, _ = h.shape
    q, f, i, g = jnp.split(h @ w_in, [HG_QK, 2 * HG_QK, 2 * HG_QK + HG_V], axis=-1)
    lb = lb.reshape(HG_H, HG_DK)
    f = f.reshape(b, L, HG_H, HG_DK).astype(jnp.float32)
    logf = jnp.log(lb + (1.0 - lb) * jax.nn.sigmoid(f))
    k = (1.0 - lb) * jax.nn.sigmoid(-f)
    q = jax.nn.silu(q.reshape(b, L, HG_H, HG_DK).astype(jnp.float32))
    v = i.reshape(b, L, HG_H, HG_DV).astype(jnp.float32)
    S, o = chunk_scan(hgrn2_chunk, S0.astype(jnp.float32), (q, k, v, logf), chunk)
    o = head_rmsnorm(o, gn_g)
    out = (jax.nn.silu(g.astype(jnp.float32)) * o) @ w_out
    return out.astype(h.dtype), S


def run_trunk(x, pos0, chunk, state_ret, cache_swa_k, cache_swa_v, state_hgrn, params):
    (norm_g, w_ff_in, w_ff_out, ret_w_in, ret_w_out, ret_gn_g, swa_w_in, swa_w_out, swa_sink,
     hg_w_in, hg_w_out, hg_gn_g, hg_lb) = params
    b = x.shape[0]
    has_past = state_ret is not None
    lbs = forget_lower_bounds(hg_lb)
    ret_out, k_out, v_out, hg_out = [], [], [], []
    for li in range(DEPTH):
        kind = li % N_MIXERS
        j = li // N_MIXERS
        x = x + 0.5 * rmsnorm(swiglu(rmsnorm(x, norm_g[li, 0]), w_ff_in[li, 0], w_ff_out[li, 0]), norm_g[li, 1])
        h = rmsnorm(x, norm_g[li, 2])
        if kind == 0:
            S0 = state_ret[j] if has_past else jnp.zeros((b, RET_H, RET_DK, RET_DV), jnp.float32)
            mix, S = retention_mixer(h, S0, pos0, chunk, ret_w_in[j], ret_w_out[j], ret_gn_g[j])
            ret_out.append(S)
        elif kind == 1:
            if has_past:
                mix, kn, vn = swa_sample(h, cache_swa_k[j], cache_swa_v[j], swa_w_in[j], swa_w_out[j], swa_sink[j])
            else:
                mix, kn, vn = swa_prompt(h, swa_w_in[j], swa_w_out[j], swa_sink[j])
            k_out.append(kn)
            v_out.append(vn)
        else:
            S0 = state_hgrn[j] if has_past else jnp.zeros((b, HG_H, HG_DK, HG_DV), jnp.float32)
            mix, S = hgrn2_mixer(h, S0, chunk, lbs[li], hg_w_in[j], hg_w_out[j], hg_gn_g[j])
            hg_out.append(S)
        x = x + rmsnorm(mix, norm_g[li, 3])
        x = x + 0.5 * rmsnorm(swiglu(rmsnorm(x, norm_g[li, 4]), w_ff_in[li, 1], w_ff_out[li, 1]), norm_g[li, 5])
    return x, jnp.stack(ret_out), jnp.stack(k_out), jnp.stack(v_out), jnp.stack(hg_out)


def setup_inputs(seed: int = 0) -> dict:
    key = jax.random.key(seed)
    ks = jax.random.split(key, 20)
    f32 = jnp.float32

    def nrm(k, shape, scale=1.0):
        return jax.random.normal(k, shape, f32) * scale

    swa_rows = min(WINDOW, PAST_LEN)
    return {
        "x_prompt": nrm(ks[0], (BATCH, SEQ, D_MODEL)),
        "x_sample": nrm(ks[1], (DEC_BATCH, DEC_SEQ, D_MODEL)),
        "state_ret": nrm(ks[2], (N_A, DEC_BATCH, RET_H, RET_DK, RET_DV)),
        "cache_swa_k": nrm(ks[3], (N_B, DEC_BATCH, swa_rows, SWA_HKV, SWA_DH)),
        "cache_swa_v": nrm(ks[4], (N_B, DEC_BATCH, swa_rows, SWA_HKV, SWA_DH)),
        "state_hgrn": nrm(ks[5], (N_C, DEC_BATCH, HG_H, HG_DK, HG_DV), 0.5),
        "norm_g": 1.0 + nrm(ks[6], (DEPTH, 6, D_MODEL), 0.05),
        "w_ff_in": nrm(ks[7], (DEPTH, 2, D_MODEL, 2 * D_FF), D_MODEL ** -0.5),
        "w_ff_out": nrm(ks[8], (DEPTH, 2, D_FF, D_MODEL), D_FF ** -0.5),
        "ret_w_in": nrm(ks[9], (N_A, D_MODEL, RET_IN), D_MODEL ** -0.5),
        "ret_w_out": nrm(ks[10], (N_A, RET_V, D_MODEL), RET_V ** -0.5),
        "ret_gn_g": 1.0 + nrm(ks[11], (N_A, RET_V), 0.05),
        "swa_w_in": nrm(ks[12], (N_B, D_MODEL, SWA_IN), D_MODEL ** -0.5),
        "swa_w_out": nrm(ks[13], (N_B, SWA_Q, D_MODEL), SWA_Q ** -0.5),
        "swa_sink": nrm(ks[14], (N_B, SWA_HQ), 0.5),
        "hg_w_in": nrm(ks[15], (N_C, D_MODEL, HG_IN), D_MODEL ** -0.5),
        "hg_w_out": nrm(ks[16], (N_C, HG_V, D_MODEL), HG_V ** -0.5),
        "hg_gn_g": 1.0 + nrm(ks[17], (N_C, HG_V), 0.05),
        "hg_lb": nrm(ks[18], (DEPTH, HG_QK), 0.5),
    }


def reference(x_prompt, x_sample, state_ret, cache_swa_k, cache_swa_v, state_hgrn, norm_g, w_ff_in, w_ff_out,
              ret_w_in, ret_w_out, ret_gn_g, swa_w_in, swa_w_out, swa_sink, hg_w_in, hg_w_out, hg_gn_g, hg_lb):
    params = (norm_g, w_ff_in, w_ff_out, ret_w_in, ret_w_out, ret_gn_g, swa_w_in, swa_w_out, swa_sink,
              hg_w_in, hg_w_out, hg_gn_g, hg_lb)
    y_prompt, ret_p, swa_k_p, swa_v_p, hg_p = run_trunk(x_prompt, 0, CHUNK, None, None, None, None, params)
    y_sample, ret_s, swa_k_s, swa_v_s, hg_s = run_trunk(x_sample, PAST_LEN, x_sample.shape[1], state_ret,
                                                        cache_swa_k, cache_swa_v, state_hgrn, params)
    return (y_prompt, y_sample, ret_p, ret_s, swa_k_p, swa_v_p, swa_k_s, swa_v_s, hg_p, hg_s)
```

```python
import math
import numpy as np
import concourse.bass as bass
import concourse.mybir as mybir
from concourse.ap import AP
from concourse.bass_utils import run_bass_kernel_spmd

F32 = mybir.dt.float32
BF16 = mybir.dt.bfloat16
AF = mybir.ActivationFunctionType
ALU = mybir.AluOpType

D = 1024
DFF = 2816
EPS = 1e-6
NSLOT = 4
SLOT = 4096
DMA_K = 12


import os as _os
_DBG = bool(_os.environ.get('K_DUMP'))


def _dsz(dt):
    return 4 if dt == F32 else 2


class Op:
    __slots__ = ("eng", "fn", "deps", "sig", "ticket", "sem", "know", "dma", "waits", "final", "dbg")


class Sched:
    def __init__(self):
        self.ops = []
        self.buf = {}

    @staticmethod
    def region(ap):
        t = ap.tensor
        shp = list(t.shape)
        ps = 1
        for s in shp[1:]:
            ps *= s
        off = ap.offset
        p0 = off // ps
        f0 = off % ps
        a = ap.ap
        npart = a[0][1]
        ext = 1
        cnt = 1
        for st, c in a[1:]:
            ext += (c - 1) * abs(st)
            cnt *= c
        dense = (cnt == ext)
        z = _dsz(t.dtype)
        return t.name, p0, p0 + npart, f0 * z, (f0 + ext) * z, dense

    def add(self, eng, fn, reads, writes, dma=False):
        op = Op()
        op.eng = eng
        op.fn = fn
        op.dma = dma
        op.deps = set()
        op.sig = False
        op.sem = None
        op.ticket = 0
        op.final = False
        for ap in reads:
            self._access(op, ap, False)
        for ap in writes:
            self._access(op, ap, True)
        self.ops.append(op)
        if _DBG:
            def nm(a):
                try:
                    return (a.tensor.name, a.offset, a.ap)
                except Exception:
                    return a
            op.dbg = (eng, dma, [nm(a) for a in reads], [nm(a) for a in writes])
        return op

    def _access(self, op, ap, isw):
        if ap is None or not isinstance(ap, AP):
            return
        if type(ap.tensor).__name__.startswith("DRam"):
            return
        name, p0, p1, f0, f1, dense = self.region(ap)
        if type(ap.tensor).__name__.startswith("PSum"):
            xp0, xp1, xf0, xf1 = (p0 // 32) * 32, ((p1 + 31) // 32) * 32, 0, 1 << 30
            psum = True
        else:
            xp0, xp1, xf0, xf1 = p0, p1, f0, f1
            psum = False
        lst = self.buf.get(name)
        if lst is None:
            lst = []
        new = []
        pe = (op.eng == "pe" and not op.dma)
        for ent in lst:
            ep0, ep1, ef0, ef1, eisw, eop, yp0, yp1, yf0, yf1 = ent
            ov = yp0 < xp1 and xp0 < yp1 and yf0 < xf1 and xf0 < yf1
            if ov and (isw or eisw or (psum and eop.eng != op.eng)) and eop is not op:
                if not (pe and eop.eng == "pe" and not eop.dma):
                    op.deps.add(eop)
            if isw and dense and p0 <= ep0 and ep1 <= p1 and f0 <= ef0 and ef1 <= f1:
                continue
            if (not isw) and (not eisw) and (not op.dma) and (not eop.dma) and eop.eng == op.eng \
                    and ep0 == p0 and ep1 == p1 and ef0 == f0 and ef1 == f1:
                continue
            new.append(ent)
        new.append((p0, p1, f0, f1, isw, op, xp0, xp1, xf0, xf1))
        self.buf[name] = new

    def finalize(self, nc, sems, dmasems):
        ops = self.ops
        dcount = {}
        dlast = {}
        for op in ops:
            if op.dma:
                q = op.eng
                i = dcount.get(q, 0)
                dcount[q] = i + 1
                sem = dmasems[q][i % DMA_K]
                op.sem = sem
                op.ticket = 16 * (i // DMA_K + 1)
                prev = dlast.get((q, i % DMA_K))
                if prev is not None:
                    op.deps.add(prev)
                dlast[(q, i % DMA_K)] = op
        for op in ops:
            for d in op.deps:
                d.sig = True
        cnt = {}
        for op in ops:
            if (not op.dma) and op.sig:
                cnt[op.eng] = cnt.get(op.eng, 0) + 1
                op.ticket = cnt[op.eng]
                op.sem = sems[op.eng]
        know = {e: {} for e in ("pe", "act", "dve", "pool", "sp")}
        for op in ops:
            kn = know[op.eng]
            need = {}
            for d in op.deps:
                s = d.sem
                if kn.get(s, 0) < d.ticket:
                    if need.get(s, (0, None))[0] < d.ticket:
                        need[s] = (d.ticket, d)
            waits = []
            for s, (tk, d) in sorted(need.items(), key=lambda kv: -kv[1][0]):
                if kn.get(s, 0) >= tk:
                    continue
                waits.append((s, tk))
                for s2, v2 in d.know.items():
                    if kn.get(s2, 0) < v2:
                        kn[s2] = v2
            op.waits = waits
            if op.sem is not None:
                k2 = dict(kn)
                k2[op.sem] = op.ticket
                op.know = k2
            else:
                op.know = None

    def emit(self, block):
        per = {e: [] for e in ("pe", "act", "dve", "pool", "sp")}
        for op in self.ops:
            per[op.eng].append(op)

        def run(lst):
            def f(e):
                for op in lst:
                    for s, v in op.waits:
                        e.wait_ge(s, v)
                    if op.fn is None:
                        continue
                    ins = op.fn(e)
                    if op.sem is not None:
                        ins.then_inc(op.sem, 16 if op.dma else 1)
            return f

        block.tensor(run(per["pe"]))
        block.scalar(run(per["act"]))
        block.vector(run(per["dve"]))
        block.gpsimd(run(per["pool"]))
        block.sync(run(per["sp"]))


def bc(ap, dims):
    return AP(ap.tensor, ap.offset, [list(ap.ap[0])] + [list(d) for d in dims])


class CstBuilder:
    def __init__(self):
        self.cols = []
        self.off = {}
        self.n = 0

    def add(self, name, arr):
        arr = np.asarray(arr, np.float32)
        if arr.ndim == 1:
            arr = arr[:, None]
        arr = arr.reshape(arr.shape[0], -1)
        if arr.shape[0] < 128:
            arr = np.concatenate([arr, np.zeros((128 - arr.shape[0], arr.shape[1]), np.float32)], 0)
        self.off[name] = self.n
        self.cols.append(arr)
        self.n += arr.shape[1]

    def build(self):
        return np.ascontiguousarray(np.concatenate(self.cols, 1))


def make_consts(T):
    cb = CstBuilder()
    cb.add("ident", np.eye(128))
    cb.add("ones", np.ones((128, 128)))
    lo = np.zeros((128, 128)); lo[:, :64] = 1
    hi = np.zeros((128, 128)); hi[:, 64:] = 1
    cb.add("ones_lo", lo)
    cb.add("ones_hi", hi)
    R = np.zeros((128, 128))
    for m in range(64):
        R[m + 64, m] = -1.0
    for m in range(64, 128):
        R[m - 64, m] = 1.0
    cb.add("rrot", R)
    gm = np.zeros((128, 232))
    for l in range(4):
        for ii in range(6):
            v = 32.0 * (0.5 if ii in (1, 5) else 1.0)
            gm[:, (l * 6 + ii) * 8:(l * 6 + ii) * 8 + 8] = v
    gm[:, 192:224] = 16.0
    gm[:, 224:232] = math.sqrt(128.0)
    cb.add("gmul", gm)
    cb.add("eps", np.tile(np.array([[1024 * EPS, 256 * EPS, 128 * EPS]]), (128, 1)))
    gam = 1.0 - 2.0 ** (-5.0 - np.arange(8, dtype=np.float64))
    lg = np.log(gam)
    s = 128.0 ** -0.5
    TG = max(T, 64)
    GW = 1024 + 1024 + 8 + 128 + TG
    geo = np.zeros((2, 128, GW))
    goff = {"D": 0, "cd": 1024, "kd": 2048, "tri": 2056, "scan": 2184, "GW": GW}
    j = np.arange(128)[:, None]
    i = np.arange(128)[None, :]
    for h in range(8):
        same = (j // 64) == (i // 64)
        ab = (j // 64 == 0) & (i // 64 == 1)
        m = np.where(same, np.exp(lg[h] * np.abs(i - j)), np.where(ab, np.exp(lg[h] * (i - j)), 0.0))
        geo[0, :, h * 128:(h + 1) * 128] = m * s
        geo[0, :, 1024 + h * 128:1024 + (h + 1) * 128] = np.exp(lg[h] * (np.arange(128) + 1.0))[None, :]
        geo[0, :, 2048 + h] = s * np.exp(lg[h] * (127.0 - np.arange(128)))
        j2 = np.arange(64)[:, None]
        i2 = np.arange(64)[None, :]
        m2 = np.where((j2 // 32) == (i2 // 32), np.exp(lg[h] * np.abs(i2 - j2)), 0.0)
        geo[1, :64, h * 128:h * 128 + 64] = m2 * s
        geo[1, :, 1024 + h * 128:1024 + h * 128 + 64] = np.exp(lg[h] * ((np.arange(64) % 32) + 1.0))[None, :]
        geo[1, :64, 2048 + h] = s * np.exp(lg[h] * (31.0 - (np.arange(64) % 32)))
    geo[0, :, 2056:2184] = (((j // 64) == (i // 64)) & (j <= i))
    j2 = np.arange(64)[:, None]; i2 = np.arange(64)[None, :]
    geo[1, :64, 2056:2056 + 64] = (((j2 // 32) == (i2 // 32)) & (j2 <= i2))
    sm = np.ones((128, TG)); sm[:, ::64] = 0
    geo[0, :, 2184:2184 + TG] = sm
    sm2 = np.ones((128, TG)); sm2[:, ::32] = 0
    geo[1, :, 2184:2184 + TG] = sm2
    gammas = [float(g) for g in gam]
    return cb, gammas, np.ascontiguousarray(geo.astype(np.float32)), goff


def make_rope(SEQ, past):
    half = 64
    inv = (np.float32(10000.0) ** (-np.arange(half, dtype=np.float32) / np.float32(half))).astype(np.float32)
    pos = np.concatenate([np.arange(SEQ, dtype=np.float32),
                          past + np.arange(32, dtype=np.float32), past + np.arange(32, dtype=np.float32)])
    ang = (pos[None, :].astype(np.float32) * inv[:, None]).astype(np.float32).astype(np.float64)
    c = np.cos(ang); sn = np.sin(ang)
    tab = np.zeros((128, 2, pos.shape[0]), np.float32)
    tab[:64, 0] = c; tab[64:, 0] = c
    tab[:64, 1] = sn; tab[64:, 1] = sn
    return tab


def _tile_w(W, cols):
    K = W.shape[0]
    return W[:, cols].reshape(K // 128, 128, len(cols)).transpose(1, 0, 2)


def build_wstream(inp, depth):
    parts = []
    units = []
    off = [0]

    def push(a):
        a = np.ascontiguousarray(a.reshape(128, -1))
        assert a.shape[1] <= SLOT
        units.append((off[0], a.shape[1]))
        parts.append(a)
        off[0] += a.shape[1]

    def ffn(l, i):
        W = inp["w_ff_in"][l, i]
        for m in range(11):
            cols = np.concatenate([np.arange(256 * m, 256 * m + 256), DFF + np.arange(256 * m, 256 * m + 256)])
            push(_tile_w(W, cols))
        Wo = inp["w_ff_out"][l, i]
        for c in range(8):
            push(_tile_w(Wo, np.arange(128 * c, 128 * c + 128)))

    def ret(jj):
        W = inp["ret_w_in"][jj]
        for hg in range(2):
            h0 = 4 * hg
            push(_tile_w(W, np.arange(h0 * 128, h0 * 128 + 512)))
            push(_tile_w(W, 1024 + np.arange(h0 * 128, h0 * 128 + 512)))
            for u in range(2):
                push(_tile_w(W, 2048 + h0 * 256 + u * 512 + np.arange(512)))
            for u in range(2):
                push(_tile_w(W, 4096 + h0 * 256 + u * 512 + np.arange(512)))
        Wo = inp["ret_w_out"][jj]
        for u in range(4):
            push(_tile_w(Wo, np.arange(256 * u, 256 * u + 256)))

    def swa(jj):
        W = inp["swa_w_in"][jj]
        for u in range(2):
            push(_tile_w(W, np.arange(512 * u, 512 * u + 512)))
        kc = np.concatenate([1024 + np.arange(64), 1024 + np.arange(64), 1088 + np.arange(64), 1088 + np.arange(64),
                             1152 + np.arange(128)])
        push(_tile_w(W, kc))
        Wo = inp["swa_w_out"][jj]
        for u in range(2):
            push(_tile_w(Wo, np.arange(512 * u, 512 * u + 512)))

    def hg(jj):
        W = inp["hg_w_in"][jj]
        for u in range(4):
            cols = np.concatenate([np.arange(256 * u, 256 * u + 256), 1024 + np.arange(256 * u, 256 * u + 256)])
            push(_tile_w(W, cols))
        for u in range(2):
            push(_tile_w(W, 2048 + np.arange(512 * u, 512 * u + 512)))
        for u in range(2):
            push(_tile_w(W, 3072 + np.arange(512 * u, 512 * u + 512)))
        Wo = inp["hg_w_out"][jj]
        for u in range(2):
            push(_tile_w(Wo, np.arange(512 * u, 512 * u + 512)))

    for li in range(depth):
        ffn(li, 0)
        kind = li % 3
        jj = li // 3
        if kind == 0:
            ret(jj)
        elif kind == 1:
            swa(jj)
        else:
            hg(jj)
        ffn(li, 1)
    return np.ascontiguousarray(np.concatenate(parts, 1)), units


class Prog:
    def __init__(self, SEQ, T, depth, units, ncst, coff, goff, gammas, do_sample=True, past=4096):
        self.SEQ, self.T, self.depth = SEQ, T, depth
        self.units = units
        self.coff = coff
        self.goff = goff
        self.gam = gammas
        self.do_sample = do_sample
        self.ncst = ncst
        self.n_ret = (depth + 2) // 3
        self.n_swa = (depth + 1) // 3
        self.n_hg = depth // 3
        self.S = Sched()
        self.bank_i = 0
        self.sq_valid = False
        self.xn_valid = False
        self.next_pre = None
        self.post_gcol = 0
        self.ssB = self.ssC = None
        self.ss_acc = None
        self.pinned = set()
        self.tmp_i = {}

    def mm(self, out, lhsT, rhs, start=True, stop=True):
        self.S.add("pe", lambda e: e.matmul(out, lhsT, rhs, start=start, stop=stop), [lhsT, rhs], [out])

    def tr(self, out, in_, ident):
        self.S.add("pe", lambda e: e.transpose(out, in_, ident), [in_, ident], [out])

    def act(self, out, in_, func, bias=None, scale=None, accum=None):
        kw = {}
        rd = [in_]
        if bias is not None:
            kw["bias"] = bias
            rd.append(bias)
        if scale is not None:
            kw["scale"] = scale
            rd.append(scale)
        wr = [out]
        if accum is not None:
            kw["accum_out"] = accum
            wr.append(accum)
        self.S.add("act", lambda e: e.activation(out, in_, func, **kw), rd, wr)

    def tt(self, eng, out, a, b, op):
        return self.S.add(eng, lambda e: e.tensor_tensor(out, a, b, op), [a, b], [out])

    def ts(self, eng, out, a, s1, s2, op0, op1=None):
        rd = [a, s1, s2]
        if op1 is None:
            self.S.add(eng, lambda e: e.tensor_scalar(out, a, s1, None, op0), rd, [out])
        else:
            self.S.add(eng, lambda e: e.tensor_scalar(out, a, s1, s2, op0, op1), rd, [out])

    def stt(self, eng, out, a, sc, b, op0, op1):
        self.S.add(eng, lambda e: e.scalar_tensor_tensor(out, a, sc, b, op0, op1), [a, sc, b], [out])

    def cp(self, eng, out, in_):
        if eng == "act":
            self.S.add("act", lambda e: e.copy(out, in_), [in_], [out])
        else:
            self.S.add(eng, lambda e: e.tensor_copy(out, in_), [in_], [out])

    def memset(self, eng, out, v):
        self.S.add(eng, lambda e: e.memset(out, v), [], [out])

    def dma(self, q, out, in_):
        return self.S.add(q, lambda e: e.dma_start(out=out, in_=in_), [in_], [out], dma=True)

    def bank(self, pin=False):
        while True:
            i = self.bank_i % 7
            self.bank_i += 1
            if i not in self.pinned:
                break
        if pin:
            self.pinned.add(i)
        return self.ps[i]

    def unpin(self, b):
        for i, p in enumerate(self.ps):
            if p is b:
                self.pinned.discard(i)

    def tmp(self, name, n):
        i = self.tmp_i.get(name, 0)
        self.tmp_i[name] = i + 1
        return i % n

    def hbc(self, k, T):
        if k < 8:
            return self.qr[:, k, :T]
        if k < 16:
            return self.kr[:, k - 8, :T]
        if k < 20:
            return self.qc[:, k - 16, :T]
        return self.qb[:, k - 20, :T]

    def w_init(self, ntiles):
        self.w_seq = [(p_, u, off, n) for p_ in range(ntiles) for u, (off, n) in enumerate(self.units)]
        self.wb = {}
        self.w_issued = 0
        self.w_used = 0
        self.w_block = set()
        self.w_held = []

    def w_prefetch(self):
        while self.w_issued < len(self.w_seq) and self.w_issued < self.w_used + NSLOT - 1:
            if (self.w_issued % NSLOT) in self.w_block:
                break
            p_, u, off, n = self.w_seq[self.w_issued]
            sl = self.wsl[self.w_issued % NSLOT]
            if p_ == 0:
                self.dma("pool", sl[:, 0:n], self.wstream[:, off:off + n])
                if len(self.w_seq) > len(self.units):
                    self.wb[u] = self.dma("sp", self.wscr[:, off:off + n], sl[:, 0:n])
            else:
                ld = self.dma("sp", sl[:, 0:n], self.wscr[:, off:off + n])
                ld.deps.add(self.wb[u])
            self.w_issued += 1

    def w_next(self, hold=False):
        self.w_prefetch()
        assert self.w_issued > self.w_used
        idx = self.w_used % NSLOT
        sl = self.wsl[idx]
        self.w_used += 1
        if hold:
            self.w_block.add(idx)
            self.w_held.append(idx)
        self.w_prefetch()
        return sl

    def w_release(self, idx):
        self.w_block.discard(idx)
        self.w_prefetch()

    def rstd_of(self, ss, T, epsn):
        r = self.rstd[:, self.tmp("rstd", 2), :T]
        ec = self.coff["eps"] + {1024: 0, 256: 1, 128: 2}[int(round(epsn / EPS))]
        self.act(r, ss[:, :T], AF.Ln, bias=self.cst[:, ec:ec + 1])
        self.act(r, r, AF.Exp, scale=-0.5)
        return r

    def sumsq_rstd(self, sq_chunks, T, epsn):
        ss = self.bank()
        n = len(sq_chunks)
        for c, s in enumerate(sq_chunks):
            self.mm(ss[:, :T], self.ones_bf[:, :], s, start=(c == 0), stop=(c == n - 1))
        return self.rstd_of(ss, T, epsn)

    def prenorm(self, gcol, T):
        if self.xn_valid:
            self.xn_valid = False
            return
        x, xn, sq = self.x, self.xn, self.sq
        for c in range(8):
            self.act(sq[:, c, :T], x[:, c, :T], AF.Square)
        ss = self.bank()
        for c in range(8):
            self.mm(ss[:, :T], self.ones_bf[:, :], sq[:, c, :T], start=(c == 0), stop=(c == 7))
        self.act(self.ssx[:, :T], ss[:, :T], AF.Copy)
        r = self.rstd_of(ss, T, D * EPS)
        for c in range(8):
            self.stt("dve", xn[:, c, :T], x[:, c, :T], self.pvs[:, gcol + c:gcol + c + 1], r, ALU.mult, ALU.mult)

    def postnorm(self, gcol, T, after_chunk=None):
        x, yb, xn = self.x, self.yb, self.xn
        r = self.rstd_of(self.ss_acc, T, D * EPS)
        self.unpin(self.ss_acc)
        self.ss_acc = None
        npre = self.next_pre
        r2 = None
        chain_end = None
        if npre is not None:
            t1 = self.ftmp[:, self.tmp("ftmp", 4), :T]
            self.tt("dve", t1, self.ssC[:, :T], r, ALU.mult)
            self.stt("dve", t1, self.ssB[:, :T], 2.0, t1, ALU.mult, ALU.add)
            self.tt("dve", t1, t1, r, ALU.mult)
            chain_end = self.tt("dve", self.ssx[:, :T], t1, self.ssx[:, :T], ALU.add)
            self.unpin(self.ssB)
            self.unpin(self.ssC)
            self.ssB = self.ssC = None
            r2 = self.rstd_of(self.ssx, T, D * EPS)
        def xn_c(c):
            self.stt("dve", xn[:, c, :T], x[:, c, :T], self.pvs[:, npre + c:npre + c + 1], r2, ALU.mult, ALU.mult)

        for c in range(8):
            self.tt("dve", yb[:, c, :T], yb[:, c, :T], r, ALU.mult)
            self.tt("dve", x[:, c, :T], x[:, c, :T], yb[:, c, :T], ALU.add)
            if after_chunk is not None:
                after_chunk(c)
            if npre is not None:
                xn_c(c)
        self.xn_valid = npre is not None

    def evac_y(self, py, c, T):
        g = self.pvs[:, self.post_gcol + c:self.post_gcol + c + 1]
        fused = self.next_pre is not None
        self.act(self.yb[:, c, :T], py[:, :T], AF.Copy, scale=g)
        self.act(self.sq[:, c, :T], py[:, :T], AF.Square)
        if fused:
            self.act(self.ptb[:, 2 + c % 2, :T], self.yb[:, c, :T], AF.Square)
            self.tt("dve", self.ptb[:, c % 2, :T], self.x[:, c, :T], self.yb[:, c, :T], ALU.mult)
        if c == 0:
            self.ss_acc = self.bank(pin=True)
            if fused:
                self.ssB = self.bank(pin=True)
                self.ssC = self.bank(pin=True)

        def red(cc, last):
            st = (cc == 0)
            self.mm(self.ss_acc[:, :T], self.ones_bf[:, :], self.sq[:, cc, :T], start=st, stop=last)
            if fused:
                self.mm(self.ssB[:, :T], self.ones_bf[:, :], self.ptb[:, cc % 2, :T], start=st, stop=last)
                self.mm(self.ssC[:, :T], self.ones_bf[:, :], self.ptb[:, 2 + cc % 2, :T], start=st, stop=last)
        if c >= 1:
            red(c - 1, False)
        if c == 7:
            red(7, True)

    def w_out_proj(self, nunits, kchunks, ocs, T):
        zb = self.sg
        self.act_preload_ln()
        for u in range(nunits):
            sl = self.w_next()
            w = sl[:, 0:4096].rearrange("p (k n) -> p k n", k=kchunks)
            for oc in range(ocs):
                py = self.bank()
                for kc in range(kchunks):
                    self.mm(py[:, :T], w[:, kc, oc * 128:(oc + 1) * 128], zb[:, kc, :T],
                            start=(kc == 0), stop=(kc == kchunks - 1))
                self.evac_y(py, ocs * u + oc, T)

    def mark(self, name):
        import os
        if os.environ.get("K_MARK"):
            print("MARK", name, len(self.S.ops), sum(1 for o in self.S.ops if o.eng == "pe"))

    def ffn(self, li, i, T, after_chunk=None):
        self.mark("ffn%d_%d" % (li, i))
        self.prenorm((li * 6 + (0 if i == 0 else 4)) * 8, T)
        self.post_gcol = (li * 6 + (1 if i == 0 else 5)) * 8
        xn = self.xn
        for m in range(11):
            sl = self.w_next()
            w = sl[:, 0:4096].rearrange("p (k n) -> p k n", k=8)
            if m == 0:
                first = self.proj_fm_multi(w, [0, 256, 128, 384], T)
            for j in range(2):
                if m == 0:
                    pa, pu = first[2 * j], first[2 * j + 1]
                else:
                    pa = self.bank()
                    pu = self.bank()
                    for kc in range(8):
                        self.mm(pa[:, :T], w[:, kc, j * 128:(j + 1) * 128], xn[:, kc, :T], start=(kc == 0), stop=(kc == 7))
                    for kc in range(8):
                        self.mm(pu[:, :T], w[:, kc, 256 + j * 128:256 + (j + 1) * 128], xn[:, kc, :T],
                                start=(kc == 0), stop=(kc == 7))
                sa = self.ftmp[:, self.tmp("ftmp", 4), :T]
                self.act(sa, pa[:, :T], AF.Silu)
                self.tt("dve", self.hbc(2 * m + j, T), sa, pu[:, :T], ALU.mult)
        self.act_preload_ln()
        for c in range(8):
            sl = self.w_next()
            w = sl[:, 0:2816].rearrange("p (k n) -> p k n", k=22)
            py = self.bank()
            for kc in range(22):
                self.mm(py[:, :T], w[:, kc, :], self.hbc(kc, T), start=(kc == 0), stop=(kc == 21))
            self.evac_y(py, c, T)
        self.postnorm((li * 6 + (1 if i == 0 else 5)) * 8, T, after_chunk=after_chunk)

    def proj_fm(self, w, j, T):
        p = self.bank()
        for kc in range(8):
            self.mm(p[:, :T], w[:, kc, j * 128:(j + 1) * 128], self.xn[:, kc, :T], start=(kc == 0), stop=(kc == 7))
        return p

    def proj_fm_multi(self, w, cols, T):
        ps = [self.bank() for _ in cols]
        for kc in range(8):
            for p, c0 in zip(ps, cols):
                self.mm(p[:, :T], w[:, kc, c0:c0 + 128], self.xn[:, kc, :T], start=(kc == 0), stop=(kc == 7))
        return ps

    def act_preload_ln(self):
        ec = self.coff["eps"]
        self.act(self.dmy[:, 0:1], self.cst[:, ec:ec + 1], AF.Ln)

    def proj_tm(self, w, b, BS, c0, n):
        p = self.bank()
        for kc in range(8):
            self.mm(p[0:BS, 0:n], self.xn[:, kc, b * BS:(b + 1) * BS], w[:, kc, c0:c0 + n], start=(kc == 0), stop=(kc == 7))
        return p

    def retention(self, li, jj, T, smp, t0, last):
        self.mark("ret%d" % li)
        self.prenorm((li * 6 + 2) * 8, T)
        self.post_gcol = (li * 6 + 3) * 8
        BS = 64 if smp else 128
        NB = T // BS
        go = self.goff
        geo = self.geo
        Dt = geo[:, go["D"]:go["D"] + 1024].rearrange("p (h n) -> p h n", h=8)
        cdt = geo[:, go["cd"]:go["cd"] + 1024].rearrange("p (h n) -> p h n", h=8)
        kdt = geo[:, go["kd"]:go["kd"] + 8]
        if smp:
            segs = [(0, 32), (32, 32)]
            gpow = [g ** 32 for g in self.gam]
        else:
            segs = [(0, 128)]
            gpow = [g ** 128 for g in self.gam]
        Sm = self.Sret[jj]
        Sbf = self.Sbf
        zb = self.sg
        qr, kr, qc, vt, sg, oT = self.qr, self.kr, self.qc, self.vt, self.sg, self.oT
        gws = {}
        gidx = {}

        def r_proj(hg):
            for which in range(2):
                sl = self.w_next()
                w = sl[:, 0:4096].rearrange("p (k n) -> p k n", k=8)
                dst = qr if which == 0 else kr
                firstq = self.proj_fm_multi(w, [0, 128, 256, 384], T) if (hg == 0 and which == 0) else None
                for j4 in range(4):
                    pq = firstq[j4] if firstq is not None else self.proj_fm(w, j4, T)
                    qb = self.qb[:, self.tmp("qb", 3), :T]
                    self.cp("act", qb, pq[:, :T])
                    pr = self.bank()
                    self.mm(pr[:, :T], self.rrot_bf[:, :], qb)
                    t1 = self.ftmp[:, self.tmp("ftmp", 4), :T]
                    t2 = self.ftmp[:, self.tmp("ftmp", 4), :T]
                    self.tt("dve", t1, qb, self.rope[:, 0, :T], ALU.mult)
                    self.tt("dve", t2, pr[:, :T], self.rope[:, 1, :T], ALU.mult)
                    self.tt("pool", dst[:, j4, :T], t1, t2, ALU.add)
                    if which == 0:
                        h = 4 * hg + j4
                        self.tt("pool", qc[:, j4, :T].rearrange("p (b n) -> p b n", n=BS),
                                qr[:, j4, :T].rearrange("p (b n) -> p b n", n=BS),
                                bc(cdt[:, h, 0:BS], [[0, NB], [1, BS]]), ALU.mult)
            for u in range(2):
                sl = self.w_next()
                w = sl[:, 0:4096].rearrange("p (k n) -> p k n", k=8)
                for b in range(NB):
                    pv = self.proj_tm(w, b, BS, 0, 512)
                    self.cp("act" if b % 2 == 0 else "dve", vt[0:BS, b, u * 512:(u + 1) * 512], pv[0:BS, :])
            gws[hg] = [self.w_next(hold=True)[:, 0:4096].rearrange("p (k n) -> p k n", k=8) for _ in range(2)]
            gidx[hg] = self.w_held[-2:]

        def g_fill(hg, j):
            pg = self.proj_fm(gws[hg][j // 4], j % 4, T)
            self.act(sg[:, 8 * hg + j, :T], pg[:, :T], AF.Silu)
            if j % 4 == 3:
                self.w_release(gidx[hg][j // 4])

        def r_core(hg):
            cpb = 8 // NB
            if not smp:
                self.cp("act", Sbf[:, 4 * hg:4 * hg + 4, :], Sm[:, 4 * hg:4 * hg + 4, :])
            for b in range(NB):
                c0b = b * BS
                Ps, kts = [], []
                for j4 in range(4):
                    h = 4 * hg + j4
                    pst = self.bank()
                    self.mm(pst[0:BS, 0:BS], kr[:, j4, c0b:c0b + BS], qr[:, j4, c0b:c0b + BS])
                    P = self.pbuf[:, self.tmp("pbuf", 4), :]
                    self.tt("dve", P[0:BS, 0:BS], pst[0:BS, 0:BS], Dt[0:BS, h, 0:BS], ALU.mult)
                    Ps.append(P)
                    o2 = self.tmp("psb", 4) * 128
                    self.tr(self.psb[0:BS, o2:o2 + 128], kr[:, j4, c0b:c0b + BS], self.ident_bf[:, :])
                    kt = self.ktb[:, self.tmp("ktb", 4), :]
                    self.act(kt[0:BS, :], self.psb[0:BS, o2:o2 + 128], AF.Copy, scale=kdt[0:BS, h:h + 1])
                    kts.append(kt)
                for ci in range(cpb):
                    g_fill(hg, b * cpb + ci)
                pos = [self.bank(pin=True), self.bank(pin=True)]
                for si, (s0, sn) in enumerate(segs):
                    if smp:
                        self.dma("sp", Sm[:, 4 * hg:4 * hg + 4, :],
                                 self.state_ret[jj, si, 4 * hg:4 * hg + 4].rearrange("h d e -> d h e"))
                        self.cp("act", Sbf[:, 4 * hg:4 * hg + 4, :], Sm[:, 4 * hg:4 * hg + 4, :])
                    for j4 in range(4):
                        h = 4 * hg + j4
                        po = pos[j4 // 2]
                        for ec in range(2):
                            ocol = ((j4 % 2) * 2 + ec) * 128
                            self.mm(po[:, ocol + s0:ocol + s0 + sn],
                                    vt[s0:s0 + sn, b, j4 * 256 + ec * 128:j4 * 256 + (ec + 1) * 128],
                                    Ps[j4][s0:s0 + sn, s0:s0 + sn], start=True, stop=False)
                            self.mm(po[:, ocol + s0:ocol + s0 + sn],
                                    Sbf[:, h, ec * 128:(ec + 1) * 128],
                                    qc[:, j4, c0b + s0:c0b + s0 + sn], start=False, stop=True)
                    for j4 in range(4):
                        h = 4 * hg + j4
                        pd = self.bank()
                        self.mm(pd[:, 0:256], kts[j4][s0:s0 + sn, :], vt[s0:s0 + sn, b, j4 * 256:(j4 + 1) * 256])
                        self.stt("dve", Sm[:, h, :], Sm[:, h, :], float(gpow[h]), pd[:, 0:256], ALU.mult, ALU.add)
                        if (not smp) and b < NB - 1:
                            self.cp("act", Sbf[:, h, :], Sm[:, h, :])
                    if smp:
                        self.outs.append(self.dma("sp", self.o_ret_s[jj, si, 4 * hg:4 * hg + 4].rearrange("h d e -> d h e"),
                                                  Sm[:, 4 * hg:4 * hg + 4, :]))
                for pi in range(2):
                    self.cp("dve", oT[:, pi * 4:pi * 4 + 4, c0b:c0b + BS],
                            pos[pi][:, 0:512].rearrange("p (c n) -> p c n", c=4)[:, :, 0:BS])
                    self.unpin(pos[pi])

        def r_norm(hg):
            osq = self.sq
            for j4 in range(4):
                self.act(osq[:, 2 * j4:2 * j4 + 2, :T], oT[:, 2 * j4:2 * j4 + 2, :T], AF.Square)
            for j4 in range(4):
                h = 4 * hg + j4
                r = self.sumsq_rstd([osq[:, 2 * j4, :T], osq[:, 2 * j4 + 1, :T]], T, 256 * EPS)
                for ec in range(2):
                    c = 2 * j4 + ec
                    self.tt("pool", oT[:, c, :T], oT[:, c, :T], r, ALU.mult)
                    gcol = 192 + jj * 16 + h * 2 + ec
                    zc = h * 2 + ec
                    self.stt("dve", zb[:, zc, :T], oT[:, c, :T], self.pvs[:, gcol:gcol + 1], sg[:, zc, :T],
                             ALU.mult, ALU.mult)

        r_proj(0)
        r_core(0)
        r_proj(1)
        r_norm(0)
        r_core(1)
        r_norm(1)
        if (not smp) and last:
            self.outs.append(self.dma("sp", self.o_ret_p[jj].rearrange("h d e -> d h e"), Sm[:, :, :]))
        self.w_out_proj(4, 16, 2, T)
        self.postnorm((li * 6 + 3) * 8, T)

    def swa_core(self, T, qcols, nq, kgroups, zcols):
        qs, vlo, vhi, zb = self.qr, self.vlo, self.vhi, self.sg
        blks = sorted(set(g[0] for g in kgroups))
        for g in range(2):
            pts = {}
            for blk in blks:
                rmax = max(r1 for (bb, r0, r1) in kgroups if bb == blk)
                rmax = 64 if rmax <= 64 else 128
                psc = self.bank()
                for h8 in range(8):
                    head = g * 8 + h8
                    kz = self.kl if head % 2 == 0 else self.kh
                    self.mm(psc[0:rmax, h8 * nq:(h8 + 1) * nq], kz[:, g, blk * 128:blk * 128 + rmax],
                            qs[:, head // 2, qcols:qcols + nq])
                pt = self.ptb[:, self.tmp("ptb", 4), :]
                self.act(pt[0:rmax, 0:8 * nq], psc[0:rmax, 0:8 * nq], AF.Exp, scale=0.125)
                pts[blk] = pt
            po = self.bank()
            pden = self.bank()
            terms = [(blk, r0, r1, par) for (blk, r0, r1) in kgroups for par in range(2)]
            for hp in range(4):
                for ti, (blk, r0, r1, par) in enumerate(terms):
                    vm = vlo if par == 0 else vhi
                    rhs = pts[blk][r0:r1, (2 * hp + par) * nq:(2 * hp + par + 1) * nq]
                    self.mm(po[:, hp * nq:(hp + 1) * nq], vm[r0:r1, blk, g, :], rhs,
                            start=(ti == 0), stop=(ti == len(terms) - 1))
                for ti, (blk, r0, r1, par) in enumerate(terms):
                    om = self.ones_lo_bf if par == 0 else self.ones_hi_bf
                    rhs = pts[blk][r0:r1, (2 * hp + par) * nq:(2 * hp + par + 1) * nq]
                    self.mm(pden[:, hp * nq:(hp + 1) * nq], om[r0:r1, :], rhs,
                            start=(ti == 0), stop=(ti == len(terms) - 1))
            dt = self.ftmp[:, self.tmp("ftmp", 4), 0:4 * nq]
            d3 = dt.rearrange("p (a n) -> p a n", a=4)
            self.tt("dve", d3, pden[:, 0:4 * nq].rearrange("p (a n) -> p a n", a=4),
                    bc(self.es[:, g * 4:g * 4 + 1], [[1, 4], [0, nq]]), ALU.add)
            self.S.add("dve", lambda e, dt=dt: e.reciprocal(dt, dt), [dt], [dt])
            self.tt("dve", zb[:, g * 4:g * 4 + 4, zcols:zcols + nq], po[:, 0:4 * nq].rearrange("p (a n) -> p a n", a=4),
                    d3, ALU.mult)

    def swa(self, li, jj, T, smp, t0, last):
        self.mark("swa%d" % li)
        self.prenorm((li * 6 + 2) * 8, T)
        self.post_gcol = (li * 6 + 3) * 8
        qs, kl, kh, vlo, vhi = self.qr, self.kl, self.kh, self.vlo, self.vhi
        BS = 64 if smp else 128
        NB = T // BS
        for u in range(2):
            sl = self.w_next()
            w = sl[:, 0:4096].rearrange("p (k n) -> p k n", k=8)
            firstq = self.proj_fm_multi(w, [0, 128, 256, 384], T) if u == 0 else None
            for j4 in range(4):
                pq = firstq[j4] if firstq is not None else self.proj_fm(w, j4, T)
                self.cp("act" if j4 % 2 == 0 else "dve", qs[:, u * 4 + j4, :T], pq[:, :T])
        sl = self.w_next()
        w = sl[:, 0:3072].rearrange("p (k n) -> p k n", k=8)
        want_out = smp or last
        n = T if smp else 128
        for g in range(2):
            pk = self.proj_fm(w, g, T)
            self.cp("act", kl[0:64, g, 128:128 + T], pk[0:64, :T])
            self.cp("act", kh[64:128, g, 128:128 + T], pk[64:128, :T])
            if want_out:
                self.cp("dve", self.kout[:, g, 0:n], pk[:, T - n:T])
        if want_out:
            dst = self.o_k_s if smp else self.o_k_p
            self.outs.append(self.dma("sp", dst[jj].rearrange("g d t -> d g t"), self.kout[0:64, :, 0:n]))
        for b in range(NB):
            pv = self.proj_tm(w, b, BS, 256, 128)
            for g in range(2):
                self.cp("act", vlo[0:BS, b + 1, g, 0:64], pv[0:BS, g * 64:(g + 1) * 64])
                self.cp("dve", vhi[0:BS, b + 1, g, 64:128], pv[0:BS, g * 64:(g + 1) * 64])
            if want_out and (smp or b == NB - 1):
                self.cp("dve", self.vout[0:BS, :], pv[0:BS, 0:128])
                dst = self.o_v_s if smp else self.o_v_p
                self.outs.append(self.dma("sp", dst[jj], self.vout[0:BS, :]))
        if smp:
            for bb in range(2):
                for g in range(2):
                    self.dma("sp", self.kc32[:, g, 0:64], self.cache_k[jj, bb, :, g, :])
                    self.dma("sp", self.kc32[:, g, 64:128], self.cache_k[jj, bb, :, g, :])
                    self.dma("pool", vlo[:, 0, g, 0:64], self.cache_v[jj, bb, :, g, :])
                    self.dma("pool", vhi[:, 0, g, 64:128], self.cache_v[jj, bb, :, g, :])
                for g in range(2):
                    pt = self.bank()
                    self.tr(pt[:, 0:128], self.kc32[:, g, :], self.cst[:, self.coff["ident"]:self.coff["ident"] + 128])
                    self.cp("act", kl[0:64, g, 0:128], pt[0:64, 0:128])
                    self.cp("act", kh[64:128, g, 0:128], pt[64:128, 0:128])
                self.swa_core(T, 32 * bb, 32, [(0, 0, 128), (1, 32 * bb, 32 * bb + 32)], 32 * bb)
        else:
            for cl in range(T // 64):
                kg = []
                for e in (cl, cl + 1, cl + 2):
                    if t0 == 0 and e < 2:
                        continue
                    kg.append((e // 2, (e % 2) * 64, (e % 2) * 64 + 64))
                mg = []
                for g_ in kg:
                    if mg and mg[-1][0] == g_[0] and mg[-1][2] == g_[1]:
                        mg[-1] = (g_[0], mg[-1][1], g_[2])
                    else:
                        mg.append(g_)
                self.swa_core(T, cl * 64, 64, mg, cl * 64)
            self.cp("pool", kl[:, :, 0:128], kl[:, :, T:T + 128])
            self.cp("pool", kh[:, :, 0:128], kh[:, :, T:T + 128])
            self.cp("pool", vlo[:, 0, :, :], vlo[:, NB, :, :])
            self.cp("pool", vhi[:, 0, :, :], vhi[:, NB, :, :])
        self.w_out_proj(2, 8, 4, T)
        self.postnorm((li * 6 + 3) * 8, T)

    def hgrn(self, li, jj, T, smp, t0, last):
        self.mark("hgrn%d" % li)
        self.prenorm((li * 6 + 2) * 8, T)
        self.post_gcol = (li * 6 + 3) * 8
        BS = 64 if smp else 128
        CL = BS // 2
        NB = T // BS
        NCH = T // CL
        go = self.goff
        tri = self.geo[:, go["tri"]:go["tri"] + 128]
        smask = self.geo[:, go["scan"]:go["scan"] + T]
        qt, ktl = self.qr, self.kr
        ebl, ebr, eref = self.ebl, self.ebr, self.eref
        Sb = self.Shg
        oT, vt, sg = self.oT, self.vt, self.sg
        for u in range(4):
            sl = self.w_next()
            w = sl[:, 0:4096].rearrange("p (k n) -> p k n", k=8)
            tls = [[oT[:, i, :T] for i in range(6)],
                   [oT[:, 6, :T], oT[:, 7, :T]] + [self.ftmp[:, i, :T] for i in range(4)]]
            hs = [2 * u, 2 * u + 1]
            if u == 0:
                four = self.proj_fm_multi(w, [0, 128, 256, 384], T)
                pqs, pfs = four[0:2], four[2:4]
            else:
                pqs = [self.proj_fm(w, j2, T) for j2 in range(2)]
                pfs = [self.proj_fm(w, 2 + j2, T) for j2 in range(2)]
            for j2 in range(2):
                self.act(tls[j2][0], pqs[j2][:, :T], AF.Silu)
            for j2 in range(2):
                self.act(tls[j2][1], pfs[j2][:, :T], AF.Sigmoid)
            for j2 in range(2):
                h = hs[j2]
                qsil, sig, lgf, kf, bcum, dd = tls[j2]
                self.act(lgf, sig, AF.Ln, bias=self.lb[:, h:h + 1], scale=self.oml[:, h:h + 1])
                self.ts("dve", kf, sig, self.noml[:, h:h + 1], self.oml[:, h:h + 1], ALU.mult, ALU.add)
                self.S.add("dve", lambda e, o=bcum, m=smask, l=lgf: e.tensor_tensor_scan(o, m, l, 0.0, ALU.mult, ALU.add),
                           [smask, lgf], [bcum])
                b3 = bcum.rearrange("p (c n) -> p c n", n=CL)
                refv = b3[:, :, CL // 2:CL // 2 + 1]
                blv = b3[:, :, CL - 1:CL]
                self.tt("dve", dd.rearrange("p (c n) -> p c n", n=CL), b3,
                        bc(bcum[:, CL // 2:CL // 2 + 1], [[CL, NCH], [0, CL]]), ALU.subtract)
                sm = self.hsm[:, j2, 0:NCH]
                self.tt("dve", sm.rearrange("p (c n) -> p c n", n=1), blv, refv, ALU.subtract)
            for j2 in range(2):
                h = hs[j2]
                qsil, sig, lgf, kf, bcum, dd = tls[j2]
                b3 = bcum.rearrange("p (c n) -> p c n", n=CL)
                refv = b3[:, :, CL // 2:CL // 2 + 1]
                blv = b3[:, :, CL - 1:CL]
                sm = self.hsm[:, j2, 0:NCH]
                self.act(ebl[:, h, 0:NCH].rearrange("p (c n) -> p c n", n=1), blv, AF.Exp)
                self.act(ebr[:, h, 0:NCH], sm, AF.Exp)
                self.act(eref[:, h, 0:NCH].rearrange("p (c n) -> p c n", n=1), refv, AF.Exp)
                self.act(lgf, dd, AF.Exp)
                self.act(sig, dd, AF.Exp, scale=-1.0)
                self.tt("pool", qt[:, h, :T], qsil, lgf, ALU.mult)
                self.tt("pool", ktl[:, h, :T], kf, sig, ALU.mult)
        for u in range(2):
            sl = self.w_next()
            w = sl[:, 0:4096].rearrange("p (k n) -> p k n", k=8)
            for b in range(NB):
                pv = self.proj_tm(w, b, BS, 0, 512)
                self.cp("act" if b % 2 == 0 else "dve", vt[0:BS, b, u * 512:(u + 1) * 512], pv[0:BS, :])
        hgw = [self.w_next(hold=True)[:, 0:4096].rearrange("p (k n) -> p k n", k=8) for _ in range(2)]
        hgi = self.w_held[-2:]
        cpg = 8 // (NB * 2)
        gcnt = [0]
        Shb = self.Sbf
        for b in range(NB):
            c0b = b * BS
            for hh in range(0, 8, 4):
                Ps, kts = [], []
                for h in range(hh, hh + 4):
                    pst = self.bank()
                    self.mm(pst[0:BS, 0:BS], ktl[:, h, c0b:c0b + BS], qt[:, h, c0b:c0b + BS])
                    P = self.pbuf[:, self.tmp("pbuf", 4), :]
                    self.tt("dve", P[0:BS, 0:BS], pst[0:BS, 0:BS], tri[0:BS, 0:BS], ALU.mult)
                    Ps.append(P)
                    o2 = self.tmp("psb", 4) * 128
                    self.tr(self.psb[0:BS, o2:o2 + 128], ktl[:, h, c0b:c0b + BS], self.ident_bf[:, :])
                    kt = self.ktb[:, self.tmp("ktb", 4), :]
                    self.cp("act", kt[0:BS, :], self.psb[0:BS, o2:o2 + 128])
                    kts.append(kt)
                for _ in range(cpg):
                    j = gcnt[0]
                    gcnt[0] += 1
                    pg = self.proj_fm(hgw[j // 4], j % 4, T)
                    self.act(sg[:, j, :T], pg[:, :T], AF.Silu)
                    if j % 4 == 3:
                        self.w_release(hgi[j // 4])
                po = self.bank(pin=True)
                for ci in range(2):
                    r0 = ci * CL
                    ch = b * 2 + ci
                    if smp:
                        self.dma("sp", Sb[:, hh:hh + 4, :], self.state_hg[jj, ci, hh:hh + 4].rearrange("h d e -> d h e"))
                    for i4, h in enumerate(range(hh, hh + 4)):
                        sbs = Shb[:, h, 0:128]
                        self.act(sbs, Sb[:, h, :], AF.Copy, scale=eref[:, h, ch:ch + 1])
                        oc = i4 * 128 + r0
                        self.mm(po[:, oc:oc + CL], vt[r0:r0 + CL, b, h * 128:(h + 1) * 128],
                                Ps[i4][r0:r0 + CL, r0:r0 + CL], start=True, stop=False)
                        self.mm(po[:, oc:oc + CL], sbs, qt[:, h, c0b + r0:c0b + r0 + CL], start=False, stop=True)
                        pd = self.bank()
                        self.mm(pd[:, 0:128], kts[i4][r0:r0 + CL, :], vt[r0:r0 + CL, b, h * 128:(h + 1) * 128])
                        t1 = self.pd32[:, self.tmp("pd32", 2), :]
                        self.ts("dve", t1, pd[:, 0:128], ebr[:, h, ch:ch + 1], None, ALU.mult)
                        self.stt("dve", Sb[:, h, :], Sb[:, h, :], ebl[:, h, ch:ch + 1], t1, ALU.mult, ALU.add)
                    if smp:
                        self.outs.append(self.dma("sp", self.o_hg_s[jj, ci, hh:hh + 4].rearrange("h d e -> d h e"),
                                                  Sb[:, hh:hh + 4, :]))
                self.cp("dve", oT[:, hh:hh + 4, c0b:c0b + BS],
                        po[:, 0:512].rearrange("p (c n) -> p c n", c=4)[:, :, 0:BS])
                self.unpin(po)
        if (not smp) and last:
            self.outs.append(self.dma("sp", self.o_hg_p[jj].rearrange("h d e -> d h e"), Sb[:, :, :]))
        osq = self.sq
        zb = self.sg
        for h in range(8):
            self.act(osq[:, h, :T], oT[:, h, :T], AF.Square)
        for h in range(8):
            r = self.sumsq_rstd([osq[:, h, :T]], T, 128 * EPS)
            self.tt("pool", oT[:, h, :T], oT[:, h, :T], r, ALU.mult)
            gcol = 224 + h
            self.stt("dve", zb[:, h, :T], oT[:, h, :T], self.pvs[:, gcol:gcol + 1], sg[:, h, :T], ALU.mult, ALU.mult)
        self.w_out_proj(2, 8, 4, T)
        self.postnorm((li * 6 + 3) * 8, T)

    def load_x_chunk(self, t0, T, smp, c):
        xsrc = self.xsT if smp else self.xT
        c0 = 0 if smp else t0
        self.dma("sp", self.x[:, c, :T], xsrc[c * 128:(c + 1) * 128, c0:c0 + T])

    def load_aux(self, t0, T, smp):
        r0 = self.SEQ if smp else t0
        self.dma("sp", self.rope[:, :, :T], self.ropeT[:, :, r0:r0 + T])
        if smp:
            self.dma("sp", self.geo[:, :], self.geod[1])

    def tile(self, t0, T, smp, last, nxt, preloaded):
        self.sq_valid = False
        c0 = 0 if smp else t0
        if not preloaded:
            for c in range(8):
                self.load_x_chunk(t0, T, smp, c)
            self.load_aux(t0, T, smp)
        ydst = self.ysT if smp else self.yT

        def fin_chunk(c):
            self.outs.append(self.dma("sp", ydst[c * 128:(c + 1) * 128, c0:c0 + T], self.x[:, c, :T]))
            if nxt is not None:
                if c == 0:
                    self.load_aux(*nxt)
                self.load_x_chunk(nxt[0], nxt[1], nxt[2], c)

        self.xn_valid = False
        for li in range(self.depth):
            self.next_pre = (li * 6 + 2) * 8
            self.ffn(li, 0, T)
            kind, jj = li % 3, li // 3
            self.next_pre = (li * 6 + 4) * 8
            if kind == 0:
                self.retention(li, jj, T, smp, t0, last)
            elif kind == 1:
                self.swa(li, jj, T, smp, t0, last)
            else:
                self.hgrn(li, jj, T, smp, t0, last)
            self.next_pre = ((li + 1) * 6) * 8 if li + 1 < self.depth else None
            self.ffn(li, 1, T, after_chunk=(fin_chunk if li == self.depth - 1 else None))

    def build(self):
        nc = bass.Bass("TRN2", target_bir_lowering=False)
        self.nc = nc
        SEQ, T = self.SEQ, self.T
        nr, ns, nh = max(self.n_ret, 1), max(self.n_swa, 1), max(self.n_hg, 1)
        TOT = sum(n for _, n in self.units)
        GW = self.goff["GW"]

        def din(name, shape):
            return nc.dram_tensor(name, list(shape), F32, kind="ExternalInput").ap()

        def dout(name, shape):
            return nc.dram_tensor(name, list(shape), F32, kind="ExternalOutput").ap()

        self.xT = din("xT", [D, SEQ])
        self.xsT = din("xsT", [D, 64])
        self.wstream = din("wstream", [128, TOT])
        self.wscr = nc.dram_tensor("wscr", [128, TOT], BF16, kind="Internal").ap()
        self.cstd = din("cst", [128, self.ncst])
        self.geod = din("geo", [2, 128, GW])
        self.pvd = din("pv", [128, 264])
        self.ropeT = din("ropeT", [128, 2, SEQ + 64])
        self.sinkd = din("sinkpp", [128, 8])
        self.state_ret = din("state_ret", [nr, 2, 8, 128, 256])
        self.state_hg = din("state_hg", [nh, 2, 8, 128, 128])
        self.cache_k = din("cache_k", [ns, 2, 128, 2, 64])
        self.cache_v = din("cache_v", [ns, 2, 128, 2, 64])
        self.yT = dout("yT", [D, SEQ])
        self.ysT = dout("ysT", [D, 64])
        self.o_ret_p = dout("o_ret_p", [nr, 8, 128, 256])
        self.o_ret_s = dout("o_ret_s", [nr, 2, 8, 128, 256])
        self.o_k_p = dout("o_k_p", [ns, 2, 64, 128])
        self.o_v_p = dout("o_v_p", [ns, 128, 128])
        self.o_k_s = dout("o_k_s", [ns, 2, 64, 64])
        self.o_v_s = dout("o_v_s", [ns, 64, 128])
        self.o_hg_p = dout("o_hg_p", [nh, 8, 128, 128])
        self.o_hg_s = dout("o_hg_s", [nh, 2, 8, 128, 128])

        import contextlib
        with contextlib.ExitStack() as es:
            def sb(name, shape, dt):
                return es.enter_context(nc.sbuf_tensor(name, list(shape), dt))

            def pst(name, shape, dt):
                return es.enter_context(nc.psum_tensor(name, list(shape), dt))

            TW = max(T, 256)
            self.cst = sb("cst_sb", [128, self.ncst], F32)
            self.geo = sb("geo_sb", [128, GW], F32)
            self.pv = sb("pv_sb", [128, 264], F32)
            self.pvs = sb("pvs", [128, 232], F32)
            self.ident_bf = sb("ident_bf", [128, 128], BF16)
            self.ones_bf = sb("ones_bf", [128, 128], BF16)
            self.ones_lo_bf = sb("ones_lo_bf", [128, 128], BF16)
            self.ones_hi_bf = sb("ones_hi_bf", [128, 128], BF16)
            self.rrot_bf = sb("rrot_bf", [128, 128], BF16)
            self.lbe = sb("lbe", [128, 4, 8], F32)
            self.lb = sb("lb", [128, 8], F32)
            self.oml = sb("oml", [128, 8], F32)
            self.noml = sb("noml", [128, 8], F32)
            self.lbt = sb("lbt", [128, 2, 8], F32)
            self.es = sb("es", [128, 8], F32)
            self.x = sb("x", [128, 8, T], F32)
            self.xn = sb("xn", [128, 8, T], BF16)
            self.rstd = sb("rstd", [128, 2, T], F32)
            self.rope = sb("rope", [128, 2, T], F32)
            self.oT = sb("oT", [128, 8, T], F32)
            self.yb = self.oT
            self.sq = sb("sq", [128, 8, T], BF16)
            self.ftmp = sb("ftmp", [128, 4, TW], F32)
            self.qb = sb("qb", [128, 3, T], BF16)
            self.qr = sb("qr", [128, 8, T], BF16)
            self.kr = sb("kr", [128, 8, T], BF16)
            self.qc = sb("qc", [128, 4, T], BF16)
            self.sg = sb("sg", [128, 16, T], BF16)
            self.vt = sb("vt", [128, max(T // 128, 1), 1024], BF16)
            self.pbuf = sb("pbuf", [128, 4, 128], BF16)
            self.ktb = sb("ktb", [128, 4, 128], BF16)
            self.ptb = sb("ptb", [128, 4, 512], BF16)
            self.kl = sb("kl", [128, 2, 128 + T], BF16)
            self.kh = sb("kh", [128, 2, 128 + T], BF16)
            self.vlo = sb("vlo", [128, T // 128 + 1, 2, 128], BF16)
            self.vhi = sb("vhi", [128, T // 128 + 1, 2, 128], BF16)
            self.kout = sb("kout", [128, 2, 128], F32)
            self.vout = sb("vout", [128, 128], F32)
            self.kc32 = sb("kc32", [128, 2, 128], F32)
            self.pd32 = sb("pd32", [128, 2, 128], F32)
            self.dmy = sb("dmy", [128, 4], F32)
            self.ssx = sb("ssx", [128, T], F32)
            self.hsm = sb("hsm", [128, 2, 16], F32)
            self.ebl = sb("ebl", [128, 8, 16], F32)
            self.ebr = sb("ebr", [128, 8, 16], F32)
            self.eref = sb("eref", [128, 8, 16], F32)
            self.Sret = [sb("Sret%d" % i, [128, 8, 256], F32) for i in range(nr)]
            self.Sbf = sb("Sbf", [128, 8, 256], BF16)
            self.Shg = sb("Shg", [128, 8, 128], F32)
            self.wsl = [sb("wsl%d" % i, [128, SLOT], BF16) for i in range(NSLOT)]
            self.ps = [pst("ps%d" % i, [128, 512], F32) for i in range(7)]
            self.psb = pst("psb", [128, 1024], BF16)
            sems = {e: es.enter_context(nc.semaphore("sem_" + e)) for e in ("pe", "act", "dve", "pool", "sp")}
            dmasems = {q: [es.enter_context(nc.semaphore("dsem_%s%d" % (q, i))) for i in range(DMA_K)]
                       for q in ("sp", "pool")}
            self.outs = []
            if _os.environ.get("K_MARK"):
                print("SBUF bytes remaining", nc.sbuf_bytes_remaining)
            self.prologue()
            ntiles = SEQ // T
            self.w_init(ntiles + (1 if self.do_sample else 0))
            tl = [(ti * T, T, False) for ti in range(ntiles)]
            if self.do_sample:
                tl.append((0, 64, True))
            for k, (t0_, T_, smp_) in enumerate(tl):
                nxt = tl[k + 1] if k + 1 < len(tl) else None
                self.tile(t0_, T_, smp_, smp_ or (t0_ + T_ == SEQ), nxt, k > 0)
            import os
            mx = int(os.environ.get("K_MAXOPS", "0"))
            if _DBG:
                lo, hi = [int(v) for v in os.environ["K_DUMP"].split(":")]
                for i in range(lo, min(hi, len(self.S.ops))):
                    print(i, self.S.ops[i].dbg)
            if mx > 0:
                print("TOTAL OPS", len(self.S.ops), "truncating to", mx)
                self.S.ops = self.S.ops[:mx]
                self.outs = [o for o in self.outs if o in set(self.S.ops)]
            fin = self.S.add("sp", None, [], [])
            for o in self.outs:
                fin.deps.add(o)
            self.S.finalize(nc, sems, dmasems)
            with nc.Block() as block:
                self.S.emit(block)
        return nc

    def prologue(self):
        co = self.coff
        cst = self.cst
        self.dma("sp", cst[:, :], self.cstd[:, :])
        self.dma("sp", self.geo[:, :], self.geod[0])
        self.dma("sp", self.pv[:, :], self.pvd[:, :])
        self.dma("sp", self.es[:, :], self.sinkd[:, :])
        for name, dst in (("ident", self.ident_bf), ("ones", self.ones_bf), ("ones_lo", self.ones_lo_bf),
                          ("ones_hi", self.ones_hi_bf), ("rrot", self.rrot_bf)):
            self.cp("dve", dst[:, :], cst[:, co[name]:co[name] + 128])
        self.tt("dve", self.pvs[:, :], self.pv[:, 0:232], cst[:, co["gmul"]:co["gmul"] + 232], ALU.mult)
        self.act(self.es[:, :], self.es[:, :], AF.Exp)
        lbe = self.lbe
        self.act(lbe[:, :, :], self.pv[:, 232:264].rearrange("p (l c) -> p l c", l=4), AF.Exp)
        t = self.lbt
        self.tt("dve", t[:, 0, :], lbe[:, 1, :], lbe[:, 2, :], ALU.add)
        self.tt("dve", t[:, 1, :], lbe[:, 0, :], lbe[:, 3, :], ALU.add)
        self.tt("dve", t[:, 1, :], t[:, 1, :], t[:, 0, :], ALU.add)
        self.S.add("dve", lambda e: e.reciprocal(t[:, 1, :], t[:, 1, :]), [t[:, 1, :]], [t[:, 1, :]])
        self.tt("dve", self.lb[:, :], t[:, 0, :], t[:, 1, :], ALU.mult)
        self.ts("dve", self.oml[:, :], self.lb[:, :], -1.0, 1.0, ALU.mult, ALU.add)
        self.ts("dve", self.noml[:, :], self.oml[:, :], -1.0, None, ALU.mult)
        for S_ in self.Sret:
            self.memset("pool", S_[:, :, :], 0.0)
        self.memset("pool", self.Shg[:, :, :], 0.0)
        self.memset("pool", self.vlo[:, :, :, :], 0.0)
        self.memset("pool", self.vhi[:, :, :, :], 0.0)
        self.memset("pool", self.kl[:, :, :], 0.0)
        self.memset("pool", self.kh[:, :, :], 0.0)


_CACHE = {}


def prepare_static(SEQ, T, past):
    cb, gam, geo, goff = make_consts(T)
    return cb.build(), cb.off, gam, geo, goff, make_rope(SEQ, past)


def run(inputs, SEQ, T, depth, n_cores, do_sample=True, past=4096, core_ids=None):
    inp = {k: np.asarray(v) for k, v in inputs.items()}
    cst, coff, gam, geo, goff, rope = prepare_static(SEQ, T, past)
    wstream, units = build_wstream(inp, depth)
    n_ret, n_swa, n_hg = (depth + 2) // 3, (depth + 1) // 3, depth // 3
    nr, ns, nh = max(n_ret, 1), max(n_swa, 1), max(n_hg, 1)
    pv = np.zeros((128, 264), np.float32)
    pv[:, 0:192] = inp["norm_g"].reshape(4, 6, 8, 128).transpose(3, 0, 1, 2).reshape(128, 192)
    pv[:, 192:224] = inp["ret_gn_g"].reshape(2, 16, 128).transpose(2, 0, 1).reshape(128, 32)
    pv[:, 224:232] = inp["hg_gn_g"].reshape(1, 8, 128)[0].T
    pv[:, 232:264] = inp["hg_lb"].reshape(4, 8, 128).transpose(2, 0, 1).reshape(128, 32)
    sink = inp["swa_sink"][0]
    sinkpp = np.zeros((128, 8), np.float32)
    for hp in range(8):
        sinkpp[:64, hp] = sink[2 * hp]
        sinkpp[64:, hp] = sink[2 * hp + 1]
    prog = Prog(SEQ, T, depth, units, cst.shape[1], coff, goff, gam, do_sample=do_sample, past=past)
    nc = prog.build()
    in_maps = []
    for c in range(n_cores):
        m = {
            "xT": np.ascontiguousarray(inp["x_prompt"][c, :SEQ].T),
            "xsT": np.ascontiguousarray(inp["x_sample"][2 * c:2 * c + 2].reshape(64, D).T),
            "wstream": wstream, "cst": cst, "geo": geo, "pv": pv, "ropeT": rope, "sinkpp": sinkpp,
            "state_ret": np.ascontiguousarray(inp["state_ret"][:nr, 2 * c:2 * c + 2]),
            "state_hg": np.ascontiguousarray(inp["state_hgrn"][:nh, 2 * c:2 * c + 2]),
            "cache_k": np.ascontiguousarray(inp["cache_swa_k"][:ns, 2 * c:2 * c + 2]),
            "cache_v": np.ascontiguousarray(inp["cache_swa_v"][:ns, 2 * c:2 * c + 2]),
        }
        in_maps.append(m)
    ids = list(range(n_cores)) if core_ids is None else core_ids
    res = run_bass_kernel_spmd(nc, in_maps, core_ids=ids)
    return res.results, (n_ret, n_swa, n_hg)


def assemble(results, counts, SEQ):
    n_ret, n_swa, n_hg = counts
    nco = len(results)
    y_p = np.stack([r["yT"].T for r in results], 0)
    y_s = np.concatenate([r["ysT"].T.reshape(2, 32, D) for r in results], 0)
    ret_p = np.stack([r["o_ret_p"][:n_ret] for r in results], 1)
    ret_s = np.concatenate([r["o_ret_s"][:n_ret] for r in results], 1)
    k_p = np.stack([r["o_k_p"][:n_swa].transpose(0, 3, 1, 2) for r in results], 1)
    v_p = np.stack([r["o_v_p"][:n_swa].reshape(n_swa, 128, 2, 64) for r in results], 1)
    k_s = np.concatenate([r["o_k_s"][:n_swa].transpose(0, 3, 1, 2).reshape(n_swa, 2, 32, 2, 64) for r in results], 1)
    v_s = np.concatenate([r["o_v_s"][:n_swa].reshape(n_swa, 2, 32, 2, 64) for r in results], 1)
    hg_p = np.stack([r["o_hg_p"][:n_hg] for r in results], 1)
    hg_s = np.concatenate([r["o_hg_s"][:n_hg] for r in results], 1)
    f = lambda a: np.ascontiguousarray(a, dtype=np.float32)
    return tuple(f(a) for a in (y_p, y_s, ret_p, ret_s, k_p, v_p, k_s, v_s, hg_p, hg_s))


def kernel(**inputs):
    results, counts = run(inputs, SEQ=4096, T=512, depth=4, n_cores=8)
    return assemble(results, counts, 4096)
```

```python
import math
import numpy as np
import concourse.bass as bass
import concourse.mybir as mybir
from concourse.ap import AP
from concourse.bass_utils import run_bass_kernel_spmd

F32 = mybir.dt.float32
BF16 = mybir.dt.bfloat16
AF = mybir.ActivationFunctionType
ALU = mybir.AluOpType

D = 1024
DFF = 2816
EPS = 1e-6
NSLOT = 4
SLOT = 4096
DMA_K = 12


import os as _os
_DBG = bool(_os.environ.get('K_DUMP'))


def _dsz(dt):
    return 4 if dt == F32 else 2


class Op:
    __slots__ = ("eng", "fn", "deps", "sig", "ticket", "sem", "know", "dma", "waits", "final", "dbg")


class Sched:
    def __init__(self):
        self.ops = []
        self.buf = {}

    @staticmethod
    def region(ap):
        t = ap.tensor
        shp = list(t.shape)
        ps = 1
        for s in shp[1:]:
            ps *= s
        off = ap.offset
        p0 = off // ps
        f0 = off % ps
        a = ap.ap
        npart = a[0][1]
        ext = 1
        cnt = 1
        for st, c in a[1:]:
            ext += (c - 1) * abs(st)
            cnt *= c
        dense = (cnt == ext)
        z = _dsz(t.dtype)
        return t.name, p0, p0 + npart, f0 * z, (f0 + ext) * z, dense

    def add(self, eng, fn, reads, writes, dma=False):
        op = Op()
        op.eng = eng
        op.fn = fn
        op.dma = dma
        op.deps = set()
        op.sig = False
        op.sem = None
        op.ticket = 0
        op.final = False
        for ap in reads:
            self._access(op, ap, False)
        for ap in writes:
            self._access(op, ap, True)
        self.ops.append(op)
        if _DBG:
            def nm(a):
                try:
                    return (a.tensor.name, a.offset, a.ap)
                except Exception:
                    return a
            op.dbg = (eng, dma, [nm(a) for a in reads], [nm(a) for a in writes])
        return op

    def _access(self, op, ap, isw):
        if ap is None or not isinstance(ap, AP):
            return
        if type(ap.tensor).__name__.startswith("DRam"):
            return
        name, p0, p1, f0, f1, dense = self.region(ap)
        if type(ap.tensor).__name__.startswith("PSum"):
            xp0, xp1, xf0, xf1 = (p0 // 32) * 32, ((p1 + 31) // 32) * 32, 0, 1 << 30
            psum = True
        else:
            xp0, xp1, xf0, xf1 = p0, p1, f0, f1
            psum = False
        lst = self.buf.get(name)
        if lst is None:
            lst = []
        new = []
        pe = (op.eng == "pe" and not op.dma)
        for ent in lst:
            ep0, ep1, ef0, ef1, eisw, eop, yp0, yp1, yf0, yf1 = ent
            ov = yp0 < xp1 and xp0 < yp1 and yf0 < xf1 and xf0 < yf1
            if ov and (isw or eisw or (psum and eop.eng != op.eng)) and eop is not op:
                if not (pe and eop.eng == "pe" and not eop.dma):
                    op.deps.add(eop)
            if isw and dense and p0 <= ep0 and ep1 <= p1 and f0 <= ef0 and ef1 <= f1:
                continue
            if (not isw) and (not eisw) and (not op.dma) and (not eop.dma) and eop.eng == op.eng \
                    and ep0 == p0 and ep1 == p1 and ef0 == f0 and ef1 == f1:
                continue
            new.append(ent)
        new.append((p0, p1, f0, f1, isw, op, xp0, xp1, xf0, xf1))
        self.buf[name] = new

    def finalize(self, nc, sems, dmasems):
        ops = self.ops
        dcount = {}
        dlast = {}
        for op in ops:
            if op.dma:
                q = op.eng
                i = dcount.get(q, 0)
                dcount[q] = i + 1
                sem = dmasems[q][i % DMA_K]
                op.sem = sem
                op.ticket = 16 * (i // DMA_K + 1)
                prev = dlast.get((q, i % DMA_K))
                if prev is not None:
                    op.deps.add(prev)
                dlast[(q, i % DMA_K)] = op
        for op in ops:
            for d in op.deps:
                d.sig = True
        cnt = {}
        for op in ops:
            if (not op.dma) and op.sig:
                cnt[op.eng] = cnt.get(op.eng, 0) + 1
                op.ticket = cnt[op.eng]
                op.sem = sems[op.eng]
        know = {e: {} for e in ("pe", "act", "dve", "pool", "sp")}
        for op in ops:
            kn = know[op.eng]
            need = {}
            for d in op.deps:
                s = d.sem
                if kn.get(s, 0) < d.ticket:
                    if need.get(s, (0, None))[0] < d.ticket:
                        need[s] = (d.ticket, d)
            waits = []
            for s, (tk, d) in sorted(need.items(), key=lambda kv: -kv[1][0]):
                if kn.get(s, 0) >= tk:
                    continue
                waits.append((s, tk))
                for s2, v2 in d.know.items():
                    if kn.get(s2, 0) < v2:
                        kn[s2] = v2
            op.waits = waits
            if op.sem is not None:
                k2 = dict(kn)
                k2[op.sem] = op.ticket
                op.know = k2
            else:
                op.know = None

    def emit(self, block):
        per = {e: [] for e in ("pe", "act", "dve", "pool", "sp")}
        for op in self.ops:
            per[op.eng].append(op)

        def run(lst):
            def f(e):
                for op in lst:
                    for s, v in op.waits:
                        e.wait_ge(s, v)
                    if op.fn is None:
                        continue
                    ins = op.fn(e)
                    if op.sem is not None:
                        ins.then_inc(op.sem, 16 if op.dma else 1)
            return f

        block.tensor(run(per["pe"]))
        block.scalar(run(per["act"]))
        block.vector(run(per["dve"]))
        block.gpsimd(run(per["pool"]))
        block.sync(run(per["sp"]))


def bc(ap, dims):
    return AP(ap.tensor, ap.offset, [list(ap.ap[0])] + [list(d) for d in dims])


class CstBuilder:
    def __init__(self):
        self.cols = []
        self.off = {}
        self.n = 0

    def add(self, name, arr):
        arr = np.asarray(arr, np.float32)
        if arr.ndim == 1:
            arr = arr[:, None]
        arr = arr.reshape(arr.shape[0], -1)
        if arr.shape[0] < 128:
            arr = np.concatenate([arr, np.zeros((128 - arr.shape[0], arr.shape[1]), np.float32)], 0)
        self.off[name] = self.n
        self.cols.append(arr)
        self.n += arr.shape[1]

    def build(self):
        return np.ascontiguousarray(np.concatenate(self.cols, 1))


def make_consts(T):
    cb = CstBuilder()
    cb.add("ident", np.eye(128))
    cb.add("ones", np.ones((128, 128)))
    lo = np.zeros((128, 128)); lo[:, :64] = 1
    hi = np.zeros((128, 128)); hi[:, 64:] = 1
    cb.add("ones_lo", lo)
    cb.add("ones_hi", hi)
    R = np.zeros((128, 128))
    for m in range(64):
        R[m + 64, m] = -1.0
    for m in range(64, 128):
        R[m - 64, m] = 1.0
    cb.add("rrot", R)
    gm = np.zeros((128, 232))
    for l in range(4):
        for ii in range(6):
            v = 32.0 * (0.5 if ii in (1, 5) else 1.0)
            gm[:, (l * 6 + ii) * 8:(l * 6 + ii) * 8 + 8] = v
    gm[:, 192:224] = 16.0
    gm[:, 224:232] = math.sqrt(128.0)
    cb.add("gmul", gm)
    cb.add("eps", np.tile(np.array([[1024 * EPS, 256 * EPS, 128 * EPS]]), (128, 1)))
    gam = 1.0 - 2.0 ** (-5.0 - np.arange(8, dtype=np.float64))
    lg = np.log(gam)
    s = 128.0 ** -0.5
    TG = max(T, 64)
    GW = 1024 + 1024 + 8 + 128 + TG
    geo = np.zeros((2, 128, GW))
    goff = {"D": 0, "cd": 1024, "kd": 2048, "tri": 2056, "scan": 2184, "GW": GW}
    j = np.arange(128)[:, None]
    i = np.arange(128)[None, :]
    for h in range(8):
        same = (j // 64) == (i // 64)
        ab = (j // 64 == 0) & (i // 64 == 1)
        m = np.where(same, np.exp(lg[h] * np.abs(i - j)), np.where(ab, np.exp(lg[h] * (i - j)), 0.0))
        geo[0, :, h * 128:(h + 1) * 128] = m * s
        geo[0, :, 1024 + h * 128:1024 + (h + 1) * 128] = np.exp(lg[h] * (np.arange(128) + 1.0))[None, :]
        geo[0, :, 2048 + h] = s * np.exp(lg[h] * (127.0 - np.arange(128)))
        j2 = np.arange(64)[:, None]
        i2 = np.arange(64)[None, :]
        m2 = np.where((j2 // 32) == (i2 // 32), np.exp(lg[h] * np.abs(i2 - j2)), 0.0)
        geo[1, :64, h * 128:h * 128 + 64] = m2 * s
        geo[1, :, 1024 + h * 128:1024 + h * 128 + 64] = np.exp(lg[h] * ((np.arange(64) % 32) + 1.0))[None, :]
        geo[1, :64, 2048 + h] = s * np.exp(lg[h] * (31.0 - (np.arange(64) % 32)))
    geo[0, :, 2056:2184] = (((j // 64) == (i // 64)) & (j <= i))
    j2 = np.arange(64)[:, None]; i2 = np.arange(64)[None, :]
    geo[1, :64, 2056:2056 + 64] = (((j2 // 32) == (i2 // 32)) & (j2 <= i2))
    sm = np.ones((128, TG)); sm[:, ::64] = 0
    geo[0, :, 2184:2184 + TG] = sm
    sm2 = np.ones((128, TG)); sm2[:, ::32] = 0
    geo[1, :, 2184:2184 + TG] = sm2
    gammas = [float(g) for g in gam]
    return cb, gammas, np.ascontiguousarray(geo.astype(np.float32)), goff


def make_rope(SEQ, past):
    half = 64
    inv = (np.float32(10000.0) ** (-np.arange(half, dtype=np.float32) / np.float32(half))).astype(np.float32)
    pos = np.concatenate([np.arange(SEQ, dtype=np.float32),
                          past + np.arange(32, dtype=np.float32), past + np.arange(32, dtype=np.float32)])
    ang = (pos[None, :].astype(np.float32) * inv[:, None]).astype(np.float32).astype(np.float64)
    c = np.cos(ang); sn = np.sin(ang)
    tab = np.zeros((128, 2, pos.shape[0]), np.float32)
    tab[:64, 0] = c; tab[64:, 0] = c
    tab[:64, 1] = sn; tab[64:, 1] = sn
    return tab


def _tile_w(W, cols):
    K = W.shape[0]
    return W[:, cols].reshape(K // 128, 128, len(cols)).transpose(1, 0, 2)


def build_wstream(inp, depth):
    parts = []
    units = []
    off = [0]

    def push(a):
        a = np.ascontiguousarray(a.reshape(128, -1))
        assert a.shape[1] <= SLOT
        units.append((off[0], a.shape[1]))
        parts.append(a)
        off[0] += a.shape[1]

    def ffn(l, i):
        W = inp["w_ff_in"][l, i]
        for m in range(11):
            cols = np.concatenate([np.arange(256 * m, 256 * m + 256), DFF + np.arange(256 * m, 256 * m + 256)])
            push(_tile_w(W, cols))
        Wo = inp["w_ff_out"][l, i]
        for c in range(8):
            push(_tile_w(Wo, np.arange(128 * c, 128 * c + 128)))

    def ret(jj):
        W = inp["ret_w_in"][jj]
        for hg in range(2):
            h0 = 4 * hg
            push(_tile_w(W, np.arange(h0 * 128, h0 * 128 + 512)))
            push(_tile_w(W, 1024 + np.arange(h0 * 128, h0 * 128 + 512)))
            for u in range(2):
                push(_tile_w(W, 2048 + h0 * 256 + u * 512 + np.arange(512)))
            for u in range(2):
                push(_tile_w(W, 4096 + h0 * 256 + u * 512 + np.arange(512)))
        Wo = inp["ret_w_out"][jj]
        for u in range(4):
            push(_tile_w(Wo, np.arange(256 * u, 256 * u + 256)))

    def swa(jj):
        W = inp["swa_w_in"][jj]
        for u in range(2):
            push(_tile_w(W, np.arange(512 * u, 512 * u + 512)))
        kc = np.concatenate([1024 + np.arange(64), 1024 + np.arange(64), 1088 + np.arange(64), 1088 + np.arange(64),
                             1152 + np.arange(128)])
        push(_tile_w(W, kc))
        Wo = inp["swa_w_out"][jj]
        for u in range(2):
            push(_tile_w(Wo, np.arange(512 * u, 512 * u + 512)))

    def hg(jj):
        W = inp["hg_w_in"][jj]
        for u in range(4):
            cols = np.concatenate([np.arange(256 * u, 256 * u + 256), 1024 + np.arange(256 * u, 256 * u + 256)])
            push(_tile_w(W, cols))
        for u in range(2):
            push(_tile_w(W, 2048 + np.arange(512 * u, 512 * u + 512)))
        for u in range(2):
            push(_tile_w(W, 3072 + np.arange(512 * u, 512 * u + 512)))
        Wo = inp["hg_w_out"][jj]
        for u in range(2):
            push(_tile_w(Wo, np.arange(512 * u, 512 * u + 512)))

    for li in range(depth):
        ffn(li, 0)
        kind = li % 3
        jj = li // 3
        if kind == 0:
            ret(jj)
        elif kind == 1:
            swa(jj)
        else:
            hg(jj)
        ffn(li, 1)
    return np.ascontiguousarray(np.concatenate(parts, 1)), units


class Prog:
    def __init__(self, SEQ, T, depth, units, ncst, coff, goff, gammas, do_sample=True, past=4096):
        self.SEQ, self.T, self.depth = SEQ, T, depth
        self.units = units
        self.coff = coff
        self.goff = goff
        self.gam = gammas
        self.do_sample = do_sample
        self.ncst = ncst
        self.n_ret = (depth + 2) // 3
        self.n_swa = (depth + 1) // 3
        self.n_hg = depth // 3
        self.S = Sched()
        self.bank_i = 0
        self.sq_valid = False
        self.xn_valid = False
        self.next_pre = None
        self.post_gcol = 0
        self.ssB = self.ssC = None
        self.ss_acc = None
        self.pinned = set()
        self.tmp_i = {}

    def mm(self, out, lhsT, rhs, start=True, stop=True):
        self.S.add("pe", lambda e: e.matmul(out, lhsT, rhs, start=start, stop=stop), [lhsT, rhs], [out])

    def tr(self, out, in_, ident):
        self.S.add("pe", lambda e: e.transpose(out, in_, ident), [in_, ident], [out])

    def act(self, out, in_, func, bias=None, scale=None, accum=None):
        kw = {}
        rd = [in_]
        if bias is not None:
            kw["bias"] = bias
            rd.append(bias)
        if scale is not None:
            kw["scale"] = scale
            rd.append(scale)
        wr = [out]
        if accum is not None:
            kw["accum_out"] = accum
            wr.append(accum)
        self.S.add("act", lambda e: e.activation(out, in_, func, **kw), rd, wr)

    def tt(self, eng, out, a, b, op):
        return self.S.add(eng, lambda e: e.tensor_tensor(out, a, b, op), [a, b], [out])

    def ts(self, eng, out, a, s1, s2, op0, op1=None):
        rd = [a, s1, s2]
        if op1 is None:
            self.S.add(eng, lambda e: e.tensor_scalar(out, a, s1, None, op0), rd, [out])
        else:
            self.S.add(eng, lambda e: e.tensor_scalar(out, a, s1, s2, op0, op1), rd, [out])

    def stt(self, eng, out, a, sc, b, op0, op1):
        self.S.add(eng, lambda e: e.scalar_tensor_tensor(out, a, sc, b, op0, op1), [a, sc, b], [out])

    def cp(self, eng, out, in_):
        if eng == "act":
            self.S.add("act", lambda e: e.copy(out, in_), [in_], [out])
        else:
            self.S.add(eng, lambda e: e.tensor_copy(out, in_), [in_], [out])

    def memset(self, eng, out, v):
        self.S.add(eng, lambda e: e.memset(out, v), [], [out])

    def dma(self, q, out, in_):
        return self.S.add(q, lambda e: e.dma_start(out=out, in_=in_), [in_], [out], dma=True)

    def bank(self, pin=False):
        while True:
            i = self.bank_i % 7
            self.bank_i += 1
            if i not in self.pinned:
                break
        if pin:
            self.pinned.add(i)
        return self.ps[i]

    def unpin(self, b):
        for i, p in enumerate(self.ps):
            if p is b:
                self.pinned.discard(i)

    def tmp(self, name, n):
        i = self.tmp_i.get(name, 0)
        self.tmp_i[name] = i + 1
        return i % n

    def hbc(self, k, T):
        if k < 8:
            return self.qr[:, k, :T]
        if k < 16:
            return self.kr[:, k - 8, :T]
        if k < 20:
            return self.qc[:, k - 16, :T]
        return self.qb[:, k - 20, :T]

    def w_init(self, ntiles):
        self.w_seq = [(p_, u, off, n) for p_ in range(ntiles) for u, (off, n) in enumerate(self.units)]
        self.wb = {}
        self.w_issued = 0
        self.w_used = 0
        self.w_block = set()
        self.w_held = []

    def w_prefetch(self):
        while self.w_issued < len(self.w_seq) and self.w_issued < self.w_used + NSLOT - 1:
            if (self.w_issued % NSLOT) in self.w_block:
                break
            p_, u, off, n = self.w_seq[self.w_issued]
            sl = self.wsl[self.w_issued % NSLOT]
            if p_ == 0:
                self.dma("pool", sl[:, 0:n], self.wstream[:, off:off + n])
                if len(self.w_seq) > len(self.units):
                    self.wb[u] = self.dma("sp", self.wscr[:, off:off + n], sl[:, 0:n])
            else:
                ld = self.dma("sp", sl[:, 0:n], self.wscr[:, off:off + n])
                ld.deps.add(self.wb[u])
            self.w_issued += 1

    def w_next(self, hold=False):
        self.w_prefetch()
        assert self.w_issued > self.w_used
        idx = self.w_used % NSLOT
        sl = self.wsl[idx]
        self.w_used += 1
        if hold:
            self.w_block.add(idx)
            self.w_held.append(idx)
        self.w_prefetch()
        return sl

    def w_release(self, idx):
        self.w_block.discard(idx)
        self.w_prefetch()

    def rstd_of(self, ss, T, epsn):
        r = self.rstd[:, self.tmp("rstd", 2), :T]
        ec = self.coff["eps"] + {1024: 0, 256: 1, 128: 2}[int(round(epsn / EPS))]
        self.act(r, ss[:, :T], AF.Ln, bias=self.cst[:, ec:ec + 1])
        self.act(r, r, AF.Exp, scale=-0.5)
        return r

    def sumsq_rstd(self, sq_chunks, T, epsn):
        ss = self.bank()
        n = len(sq_chunks)
        for c, s in enumerate(sq_chunks):
            self.mm(ss[:, :T], self.ones_bf[:, :], s, start=(c == 0), stop=(c == n - 1))
        return self.rstd_of(ss, T, epsn)

    def prenorm(self, gcol, T):
        if self.xn_valid:
            self.xn_valid = False
            return
        x, xn, sq = self.x, self.xn, self.sq
        for c in range(8):
            self.act(sq[:, c, :T], x[:, c, :T], AF.Square)
        ss = self.bank()
        for c in range(8):
            self.mm(ss[:, :T], self.ones_bf[:, :], sq[:, c, :T], start=(c == 0), stop=(c == 7))
        self.act(self.ssx[:, :T], ss[:, :T], AF.Copy)
        r = self.rstd_of(ss, T, D * EPS)
        for c in range(8):
            self.stt("dve", xn[:, c, :T], x[:, c, :T], self.pvs[:, gcol + c:gcol + c + 1], r, ALU.mult, ALU.mult)

    def postnorm(self, gcol, T, after_chunk=None):
        x, yb, xn = self.x, self.yb, self.xn
        r = self.rstd_of(self.ss_acc, T, D * EPS)
        self.unpin(self.ss_acc)
        self.ss_acc = None
        npre = self.next_pre
        r2 = None
        chain_end = None
        if npre is not None:
            t1 = self.ftmp[:, self.tmp("ftmp", 4), :T]
            self.tt("dve", t1, self.ssC[:, :T], r, ALU.mult)
            self.stt("dve", t1, self.ssB[:, :T], 2.0, t1, ALU.mult, ALU.add)
            self.tt("dve", t1, t1, r, ALU.mult)
            chain_end = self.tt("dve", self.ssx[:, :T], t1, self.ssx[:, :T], ALU.add)
            self.unpin(self.ssB)
            self.unpin(self.ssC)
            self.ssB = self.ssC = None
            r2 = self.rstd_of(self.ssx, T, D * EPS)
        def xn_c(c):
            self.stt("dve", xn[:, c, :T], x[:, c, :T], self.pvs[:, npre + c:npre + c + 1], r2, ALU.mult, ALU.mult)

        for c in range(8):
            self.tt("dve", yb[:, c, :T], yb[:, c, :T], r, ALU.mult)
            self.tt("dve", x[:, c, :T], x[:, c, :T], yb[:, c, :T], ALU.add)
            if after_chunk is not None:
                after_chunk(c)
            if npre is not None:
                xn_c(c)
        self.xn_valid = npre is not None

    def evac_y(self, py, c, T):
        g = self.pvs[:, self.post_gcol + c:self.post_gcol + c + 1]
        fused = self.next_pre is not None
        self.act(self.yb[:, c, :T], py[:, :T], AF.Copy, scale=g)
        self.act(self.sq[:, c, :T], py[:, :T], AF.Square)
        if fused:
            self.act(self.ptb[:, 2 + c % 2, :T], self.yb[:, c, :T], AF.Square)
            self.tt("dve", self.ptb[:, c % 2, :T], self.x[:, c, :T], self.yb[:, c, :T], ALU.mult)
        if c == 0:
            self.ss_acc = self.bank(pin=True)
            if fused:
                self.ssB = self.bank(pin=True)
                self.ssC = self.bank(pin=True)

        def red(cc, last):
            st = (cc == 0)
            self.mm(self.ss_acc[:, :T], self.ones_bf[:, :], self.sq[:, cc, :T], start=st, stop=last)
            if fused:
                self.mm(self.ssB[:, :T], self.ones_bf[:, :], self.ptb[:, cc % 2, :T], start=st, stop=last)
                self.mm(self.ssC[:, :T], self.ones_bf[:, :], self.ptb[:, 2 + cc % 2, :T], start=st, stop=last)
        if c >= 1:
            red(c - 1, False)
        if c == 7:
            red(7, True)

    def w_out_proj(self, nunits, kchunks, ocs, T):
        zb = self.sg
        self.act_preload_ln()
        for u in range(nunits):
            sl = self.w_next()
            w = sl[:, 0:4096].rearrange("p (k n) -> p k n", k=kchunks)
            for oc in range(ocs):
                py = self.bank()
                for kc in range(kchunks):
                    self.mm(py[:, :T], w[:, kc, oc * 128:(oc + 1) * 128], zb[:, kc, :T],
                            start=(kc == 0), stop=(kc == kchunks - 1))
                self.evac_y(py, ocs * u + oc, T)

    def mark(self, name):
        import os
        if os.environ.get("K_MARK"):
            print("MARK", name, len(self.S.ops), sum(1 for o in self.S.ops if o.eng == "pe"))

    def ffn(self, li, i, T, after_chunk=None):
        self.mark("ffn%d_%d" % (li, i))
        self.prenorm((li * 6 + (0 if i == 0 else 4)) * 8, T)
        self.post_gcol = (li * 6 + (1 if i == 0 else 5)) * 8
        xn = self.xn
        w1 = None
        for m in range(11):
            if m == 1:
                w = w1
            else:
                sl = self.w_next(hold=(m == 0))
                w = sl[:, 0:4096].rearrange("p (k n) -> p k n", k=8)
            if m == 0:
                idx0 = self.w_held[-1]
                w1 = self.w_next()[:, 0:4096].rearrange("p (k n) -> p k n", k=8)
                first = [self.bank() for _ in range(6)]
                srcs = [(w, 0), (w, 256), (w, 128), (w, 384), (w1, 0), (w1, 256)]
                for kc in range(8):
                    for p, (wv, c0) in zip(first, srcs):
                        self.mm(p[:, :T], wv[:, kc, c0:c0 + 128], xn[:, kc, :T], start=(kc == 0), stop=(kc == 7))
                self.w_release(idx0)
            for j in range(2):
                if m == 0:
                    pa, pu = first[2 * j], first[2 * j + 1]
                elif m == 1 and j == 0:
                    pa, pu = first[4], first[5]
                else:
                    pa = self.bank()
                    pu = self.bank()
                    for kc in range(8):
                        self.mm(pa[:, :T], w[:, kc, j * 128:(j + 1) * 128], xn[:, kc, :T], start=(kc == 0), stop=(kc == 7))
                    for kc in range(8):
                        self.mm(pu[:, :T], w[:, kc, 256 + j * 128:256 + (j + 1) * 128], xn[:, kc, :T],
                                start=(kc == 0), stop=(kc == 7))
                sa = self.ftmp[:, self.tmp("ftmp", 4), :T]
                self.act(sa, pa[:, :T], AF.Silu)
                self.tt("dve", self.hbc(2 * m + j, T), sa, pu[:, :T], ALU.mult)
        self.act_preload_ln()
        for c in range(8):
            sl = self.w_next()
            w = sl[:, 0:2816].rearrange("p (k n) -> p k n", k=22)
            py = self.bank()
            for kc in range(22):
                self.mm(py[:, :T], w[:, kc, :], self.hbc(kc, T), start=(kc == 0), stop=(kc == 21))
            self.evac_y(py, c, T)
        self.postnorm((li * 6 + (1 if i == 0 else 5)) * 8, T, after_chunk=after_chunk)

    def proj_fm(self, w, j, T):
        p = self.bank()
        for kc in range(8):
            self.mm(p[:, :T], w[:, kc, j * 128:(j + 1) * 128], self.xn[:, kc, :T], start=(kc == 0), stop=(kc == 7))
        return p

    def proj_fm_multi(self, w, cols, T):
        ps = [self.bank() for _ in cols]
        for kc in range(8):
            for p, c0 in zip(ps, cols):
                self.mm(p[:, :T], w[:, kc, c0:c0 + 128], self.xn[:, kc, :T], start=(kc == 0), stop=(kc == 7))
        return ps

    def act_preload_ln(self):
        ec = self.coff["eps"]
        self.act(self.dmy[:, 0:1], self.cst[:, ec:ec + 1], AF.Ln)

    def proj_tm(self, w, b, BS, c0, n):
        p = self.bank()
        for kc in range(8):
            self.mm(p[0:BS, 0:n], self.xn[:, kc, b * BS:(b + 1) * BS], w[:, kc, c0:c0 + n], start=(kc == 0), stop=(kc == 7))
        return p

    def retention(self, li, jj, T, smp, t0, last):
        self.mark("ret%d" % li)
        self.prenorm((li * 6 + 2) * 8, T)
        self.post_gcol = (li * 6 + 3) * 8
        BS = 64 if smp else 128
        NB = T // BS
        go = self.goff
        geo = self.geo
        Dt = geo[:, go["D"]:go["D"] + 1024].rearrange("p (h n) -> p h n", h=8)
        cdt = geo[:, go["cd"]:go["cd"] + 1024].rearrange("p (h n) -> p h n", h=8)
        kdt = geo[:, go["kd"]:go["kd"] + 8]
        if smp:
            segs = [(0, 32), (32, 32)]
            gpow = [g ** 32 for g in self.gam]
        else:
            segs = [(0, 128)]
            gpow = [g ** 128 for g in self.gam]
        Sm = self.Sret[jj]
        Sbf = self.Sbf
        zb = self.sg
        qr, kr, qc, vt, sg, oT = self.qr, self.kr, self.qc, self.vt, self.sg, self.oT
        gws = {}
        gidx = {}

        def r_proj(hg):
            for which in range(2):
                sl = self.w_next()
                w = sl[:, 0:4096].rearrange("p (k n) -> p k n", k=8)
                dst = qr if which == 0 else kr
                firstq = self.proj_fm_multi(w, [0, 128, 256, 384], T) if (hg == 0 and which == 0) else None
                for j4 in range(4):
                    pq = firstq[j4] if firstq is not None else self.proj_fm(w, j4, T)
                    qb = self.qb[:, self.tmp("qb", 3), :T]
                    self.cp("act", qb, pq[:, :T])
                    pr = self.bank()
                    self.mm(pr[:, :T], self.rrot_bf[:, :], qb)
                    t1 = self.ftmp[:, self.tmp("ftmp", 4), :T]
                    t2 = self.ftmp[:, self.tmp("ftmp", 4), :T]
                    self.tt("dve", t1, qb, self.rope[:, 0, :T], ALU.mult)
                    self.tt("dve", t2, pr[:, :T], self.rope[:, 1, :T], ALU.mult)
                    self.tt("pool", dst[:, j4, :T], t1, t2, ALU.add)
                    if which == 0:
                        h = 4 * hg + j4
                        self.tt("pool", qc[:, j4, :T].rearrange("p (b n) -> p b n", n=BS),
                                qr[:, j4, :T].rearrange("p (b n) -> p b n", n=BS),
                                bc(cdt[:, h, 0:BS], [[0, NB], [1, BS]]), ALU.mult)
            for u in range(2):
                sl = self.w_next()
                w = sl[:, 0:4096].rearrange("p (k n) -> p k n", k=8)
                for b in range(NB):
                    pv = self.proj_tm(w, b, BS, 0, 512)
                    self.cp("act" if b % 2 == 0 else "dve", vt[0:BS, b, u * 512:(u + 1) * 512], pv[0:BS, :])
            gws[hg] = [self.w_next(hold=True)[:, 0:4096].rearrange("p (k n) -> p k n", k=8) for _ in range(2)]
            gidx[hg] = self.w_held[-2:]

        def g_fill(hg, j):
            pg = self.proj_fm(gws[hg][j // 4], j % 4, T)
            self.act(sg[:, 8 * hg + j, :T], pg[:, :T], AF.Silu)
            if j % 4 == 3:
                self.w_release(gidx[hg][j // 4])

        def r_core(hg):
            cpb = 8 // NB
            if not smp:
                self.cp("act", Sbf[:, 4 * hg:4 * hg + 4, :], Sm[:, 4 * hg:4 * hg + 4, :])
            for b in range(NB):
                c0b = b * BS
                Ps, kts = [], []
                for j4 in range(4):
                    h = 4 * hg + j4
                    pst = self.bank()
                    self.mm(pst[0:BS, 0:BS], kr[:, j4, c0b:c0b + BS], qr[:, j4, c0b:c0b + BS])
                    P = self.pbuf[:, self.tmp("pbuf", 4), :]
                    self.tt("dve", P[0:BS, 0:BS], pst[0:BS, 0:BS], Dt[0:BS, h, 0:BS], ALU.mult)
                    Ps.append(P)
                    o2 = self.tmp("psb", 4) * 128
                    self.tr(self.psb[0:BS, o2:o2 + 128], kr[:, j4, c0b:c0b + BS], self.ident_bf[:, :])
                    kt = self.ktb[:, self.tmp("ktb", 4), :]
                    self.act(kt[0:BS, :], self.psb[0:BS, o2:o2 + 128], AF.Copy, scale=kdt[0:BS, h:h + 1])
                    kts.append(kt)
                for ci in range(cpb):
                    g_fill(hg, b * cpb + ci)
                pos = [self.bank(pin=True), self.bank(pin=True)]
                for si, (s0, sn) in enumerate(segs):
                    if smp:
                        self.dma("sp", Sm[:, 4 * hg:4 * hg + 4, :],
                                 self.state_ret[jj, si, 4 * hg:4 * hg + 4].rearrange("h d e -> d h e"))
                        self.cp("act", Sbf[:, 4 * hg:4 * hg + 4, :], Sm[:, 4 * hg:4 * hg + 4, :])
                    for j4 in range(4):
                        h = 4 * hg + j4
                        po = pos[j4 // 2]
                        for ec in range(2):
                            ocol = ((j4 % 2) * 2 + ec) * 128
                            self.mm(po[:, ocol + s0:ocol + s0 + sn],
                                    vt[s0:s0 + sn, b, j4 * 256 + ec * 128:j4 * 256 + (ec + 1) * 128],
                                    Ps[j4][s0:s0 + sn, s0:s0 + sn], start=True, stop=False)
                            self.mm(po[:, ocol + s0:ocol + s0 + sn],
                                    Sbf[:, h, ec * 128:(ec + 1) * 128],
                                    qc[:, j4, c0b + s0:c0b + s0 + sn], start=False, stop=True)
                    for j4 in range(4):
                        h = 4 * hg + j4
                        pd = self.bank()
                        self.mm(pd[:, 0:256], kts[j4][s0:s0 + sn, :], vt[s0:s0 + sn, b, j4 * 256:(j4 + 1) * 256])
                        self.stt("dve", Sm[:, h, :], Sm[:, h, :], float(gpow[h]), pd[:, 0:256], ALU.mult, ALU.add)
                        if (not smp) and b < NB - 1:
                            self.cp("act", Sbf[:, h, :], Sm[:, h, :])
                    if smp:
                        self.outs.append(self.dma("sp", self.o_ret_s[jj, si, 4 * hg:4 * hg + 4].rearrange("h d e -> d h e"),
                                                  Sm[:, 4 * hg:4 * hg + 4, :]))
                for pi in range(2):
                    self.cp("dve", oT[:, pi * 4:pi * 4 + 4, c0b:c0b + BS],
                            pos[pi][:, 0:512].rearrange("p (c n) -> p c n", c=4)[:, :, 0:BS])
                    self.unpin(pos[pi])

        def r_norm(hg):
            osq = self.sq
            for j4 in range(4):
                self.act(osq[:, 2 * j4:2 * j4 + 2, :T], oT[:, 2 * j4:2 * j4 + 2, :T], AF.Square)
            for j4 in range(4):
                h = 4 * hg + j4
                r = self.sumsq_rstd([osq[:, 2 * j4, :T], osq[:, 2 * j4 + 1, :T]], T, 256 * EPS)
                for ec in range(2):
                    c = 2 * j4 + ec
                    self.tt("pool", oT[:, c, :T], oT[:, c, :T], r, ALU.mult)
                    gcol = 192 + jj * 16 + h * 2 + ec
                    zc = h * 2 + ec
                    self.stt("dve", zb[:, zc, :T], oT[:, c, :T], self.pvs[:, gcol:gcol + 1], sg[:, zc, :T],
                             ALU.mult, ALU.mult)

        r_proj(0)
        r_core(0)
        r_proj(1)
        r_norm(0)
        r_core(1)
        r_norm(1)
        if (not smp) and last:
            self.outs.append(self.dma("sp", self.o_ret_p[jj].rearrange("h d e -> d h e"), Sm[:, :, :]))
        self.w_out_proj(4, 16, 2, T)
        self.postnorm((li * 6 + 3) * 8, T)

    def swa_core(self, T, qcols, nq, kgroups, zcols):
        qs, vlo, vhi, zb = self.qr, self.vlo, self.vhi, self.sg
        blks = sorted(set(g[0] for g in kgroups))
        for g in range(2):
            pts = {}
            for blk in blks:
                rmax = max(r1 for (bb, r0, r1) in kgroups if bb == blk)
                rmax = 64 if rmax <= 64 else 128
                psc = self.bank()
                for h8 in range(8):
                    head = g * 8 + h8
                    kz = self.kl if head % 2 == 0 else self.kh
                    self.mm(psc[0:rmax, h8 * nq:(h8 + 1) * nq], kz[:, g, blk * 128:blk * 128 + rmax],
                            qs[:, head // 2, qcols:qcols + nq])
                pt = self.ptb[:, self.tmp("ptb", 4), :]
                self.act(pt[0:rmax, 0:8 * nq], psc[0:rmax, 0:8 * nq], AF.Exp, scale=0.125)
                pts[blk] = pt
            po = self.bank()
            pden = self.bank()
            terms = [(blk, r0, r1, par) for (blk, r0, r1) in kgroups for par in range(2)]
            for hp in range(4):
                for ti, (blk, r0, r1, par) in enumerate(terms):
                    vm = vlo if par == 0 else vhi
                    rhs = pts[blk][r0:r1, (2 * hp + par) * nq:(2 * hp + par + 1) * nq]
                    self.mm(po[:, hp * nq:(hp + 1) * nq], vm[r0:r1, blk, g, :], rhs,
                            start=(ti == 0), stop=(ti == len(terms) - 1))
                for ti, (blk, r0, r1, par) in enumerate(terms):
                    om = self.ones_lo_bf if par == 0 else self.ones_hi_bf
                    rhs = pts[blk][r0:r1, (2 * hp + par) * nq:(2 * hp + par + 1) * nq]
                    self.mm(pden[:, hp * nq:(hp + 1) * nq], om[r0:r1, :], rhs,
                            start=(ti == 0), stop=(ti == len(terms) - 1))
            dt = self.ftmp[:, self.tmp("ftmp", 4), 0:4 * nq]
            d3 = dt.rearrange("p (a n) -> p a n", a=4)
            self.tt("dve", d3, pden[:, 0:4 * nq].rearrange("p (a n) -> p a n", a=4),
                    bc(self.es[:, g * 4:g * 4 + 1], [[1, 4], [0, nq]]), ALU.add)
            self.S.add("dve", lambda e, dt=dt: e.reciprocal(dt, dt), [dt], [dt])
            self.tt("dve", zb[:, g * 4:g * 4 + 4, zcols:zcols + nq], po[:, 0:4 * nq].rearrange("p (a n) -> p a n", a=4),
                    d3, ALU.mult)

    def swa(self, li, jj, T, smp, t0, last):
        self.mark("swa%d" % li)
        self.prenorm((li * 6 + 2) * 8, T)
        self.post_gcol = (li * 6 + 3) * 8
        qs, kl, kh, vlo, vhi = self.qr, self.kl, self.kh, self.vlo, self.vhi
        BS = 64 if smp else 128
        NB = T // BS
        for u in range(2):
            sl = self.w_next()
            w = sl[:, 0:4096].rearrange("p (k n) -> p k n", k=8)
            firstq = self.proj_fm_multi(w, [0, 128, 256, 384], T) if u == 0 else None
            for j4 in range(4):
                pq = firstq[j4] if firstq is not None else self.proj_fm(w, j4, T)
                self.cp("act" if j4 % 2 == 0 else "dve", qs[:, u * 4 + j4, :T], pq[:, :T])
        sl = self.w_next()
        w = sl[:, 0:3072].rearrange("p (k n) -> p k n", k=8)
        want_out = smp or last
        n = T if smp else 128
        for g in range(2):
            pk = self.proj_fm(w, g, T)
            self.cp("act", kl[0:64, g, 128:128 + T], pk[0:64, :T])
            self.cp("act", kh[64:128, g, 128:128 + T], pk[64:128, :T])
            if want_out:
                self.cp("dve", self.kout[:, g, 0:n], pk[:, T - n:T])
        if want_out:
            dst = self.o_k_s if smp else self.o_k_p
            self.outs.append(self.dma("sp", dst[jj].rearrange("g d t -> d g t"), self.kout[0:64, :, 0:n]))
        for b in range(NB):
            pv = self.proj_tm(w, b, BS, 256, 128)
            for g in range(2):
                self.cp("act", vlo[0:BS, b + 1, g, 0:64], pv[0:BS, g * 64:(g + 1) * 64])
                self.cp("dve", vhi[0:BS, b + 1, g, 64:128], pv[0:BS, g * 64:(g + 1) * 64])
            if want_out and (smp or b == NB - 1):
                self.cp("dve", self.vout[0:BS, :], pv[0:BS, 0:128])
                dst = self.o_v_s if smp else self.o_v_p
                self.outs.append(self.dma("sp", dst[jj], self.vout[0:BS, :]))
        if smp:
            for bb in range(2):
                for g in range(2):
                    self.dma("sp", self.kc32[:, g, 0:64], self.cache_k[jj, bb, :, g, :])
                    self.dma("sp", self.kc32[:, g, 64:128], self.cache_k[jj, bb, :, g, :])
                    self.dma("pool", vlo[:, 0, g, 0:64], self.cache_v[jj, bb, :, g, :])
                    self.dma("pool", vhi[:, 0, g, 64:128], self.cache_v[jj, bb, :, g, :])
                for g in range(2):
                    pt = self.bank()
                    self.tr(pt[:, 0:128], self.kc32[:, g, :], self.cst[:, self.coff["ident"]:self.coff["ident"] + 128])
                    self.cp("act", kl[0:64, g, 0:128], pt[0:64, 0:128])
                    self.cp("act", kh[64:128, g, 0:128], pt[64:128, 0:128])
                self.swa_core(T, 32 * bb, 32, [(0, 0, 128), (1, 32 * bb, 32 * bb + 32)], 32 * bb)
        else:
            for cl in range(T // 64):
                kg = []
                for e in (cl, cl + 1, cl + 2):
                    if t0 == 0 and e < 2:
                        continue
                    kg.append((e // 2, (e % 2) * 64, (e % 2) * 64 + 64))
                mg = []
                for g_ in kg:
                    if mg and mg[-1][0] == g_[0] and mg[-1][2] == g_[1]:
                        mg[-1] = (g_[0], mg[-1][1], g_[2])
                    else:
                        mg.append(g_)
                self.swa_core(T, cl * 64, 64, mg, cl * 64)
            self.cp("pool", kl[:, :, 0:128], kl[:, :, T:T + 128])
            self.cp("pool", kh[:, :, 0:128], kh[:, :, T:T + 128])
            self.cp("pool", vlo[:, 0, :, :], vlo[:, NB, :, :])
            self.cp("pool", vhi[:, 0, :, :], vhi[:, NB, :, :])
        self.w_out_proj(2, 8, 4, T)
        self.postnorm((li * 6 + 3) * 8, T)

    def hgrn(self, li, jj, T, smp, t0, last):
        self.mark("hgrn%d" % li)
        self.prenorm((li * 6 + 2) * 8, T)
        self.post_gcol = (li * 6 + 3) * 8
        BS = 64 if smp else 128
        CL = BS // 2
        NB = T // BS
        NCH = T // CL
        go = self.goff
        tri = self.geo[:, go["tri"]:go["tri"] + 128]
        smask = self.geo[:, go["scan"]:go["scan"] + T]
        qt, ktl = self.qr, self.kr
        ebl, ebr, eref = self.ebl, self.ebr, self.eref
        Sb = self.Shg
        oT, vt, sg = self.oT, self.vt, self.sg
        for u in range(4):
            sl = self.w_next()
            w = sl[:, 0:4096].rearrange("p (k n) -> p k n", k=8)
            tls = [[oT[:, i, :T] for i in range(6)],
                   [oT[:, 6, :T], oT[:, 7, :T]] + [self.ftmp[:, i, :T] for i in range(4)]]
            hs = [2 * u, 2 * u + 1]
            if u == 0:
                four = self.proj_fm_multi(w, [0, 128, 256, 384], T)
                pqs, pfs = four[0:2], four[2:4]
            else:
                pqs = [self.proj_fm(w, j2, T) for j2 in range(2)]
                pfs = [self.proj_fm(w, 2 + j2, T) for j2 in range(2)]
            for j2 in range(2):
                self.act(tls[j2][0], pqs[j2][:, :T], AF.Silu)
            for j2 in range(2):
                self.act(tls[j2][1], pfs[j2][:, :T], AF.Sigmoid)
            for j2 in range(2):
                h = hs[j2]
                qsil, sig, lgf, kf, bcum, dd = tls[j2]
                self.act(lgf, sig, AF.Ln, bias=self.lb[:, h:h + 1], scale=self.oml[:, h:h + 1])
                self.ts("dve", kf, sig, self.noml[:, h:h + 1], self.oml[:, h:h + 1], ALU.mult, ALU.add)
                self.S.add("dve", lambda e, o=bcum, m=smask, l=lgf: e.tensor_tensor_scan(o, m, l, 0.0, ALU.mult, ALU.add),
                           [smask, lgf], [bcum])
                b3 = bcum.rearrange("p (c n) -> p c n", n=CL)
                refv = b3[:, :, CL // 2:CL // 2 + 1]
                blv = b3[:, :, CL - 1:CL]
                self.tt("dve", dd.rearrange("p (c n) -> p c n", n=CL), b3,
                        bc(bcum[:, CL // 2:CL // 2 + 1], [[CL, NCH], [0, CL]]), ALU.subtract)
                sm = self.hsm[:, j2, 0:NCH]
                self.tt("dve", sm.rearrange("p (c n) -> p c n", n=1), blv, refv, ALU.subtract)
            for j2 in range(2):
                h = hs[j2]
                qsil, sig, lgf, kf, bcum, dd = tls[j2]
                b3 = bcum.rearrange("p (c n) -> p c n", n=CL)
                refv = b3[:, :, CL // 2:CL // 2 + 1]
                blv = b3[:, :, CL - 1:CL]
                sm = self.hsm[:, j2, 0:NCH]
                self.act(ebl[:, h, 0:NCH].rearrange("p (c n) -> p c n", n=1), blv, AF.Exp)
                self.act(ebr[:, h, 0:NCH], sm, AF.Exp)
                self.act(eref[:, h, 0:NCH].rearrange("p (c n) -> p c n", n=1), refv, AF.Exp)
                self.act(lgf, dd, AF.Exp)
                self.act(sig, dd, AF.Exp, scale=-1.0)
                self.tt("pool", qt[:, h, :T], qsil, lgf, ALU.mult)
                self.tt("pool", ktl[:, h, :T], kf, sig, ALU.mult)
        for u in range(2):
            sl = self.w_next()
            w = sl[:, 0:4096].rearrange("p (k n) -> p k n", k=8)
            for b in range(NB):
                pv = self.proj_tm(w, b, BS, 0, 512)
                self.cp("act" if b % 2 == 0 else "dve", vt[0:BS, b, u * 512:(u + 1) * 512], pv[0:BS, :])
        hgw = [self.w_next(hold=True)[:, 0:4096].rearrange("p (k n) -> p k n", k=8) for _ in range(2)]
        hgi = self.w_held[-2:]
        cpg = 8 // (NB * 2)
        gcnt = [0]
        Shb = self.Sbf
        for b in range(NB):
            c0b = b * BS
            for hh in range(0, 8, 4):
                Ps, kts = [], []
                for h in range(hh, hh + 4):
                    pst = self.bank()
                    self.mm(pst[0:BS, 0:BS], ktl[:, h, c0b:c0b + BS], qt[:, h, c0b:c0b + BS])
                    P = self.pbuf[:, self.tmp("pbuf", 4), :]
                    self.tt("dve", P[0:BS, 0:BS], pst[0:BS, 0:BS], tri[0:BS, 0:BS], ALU.mult)
                    Ps.append(P)
                    o2 = self.tmp("psb", 4) * 128
                    self.tr(self.psb[0:BS, o2:o2 + 128], ktl[:, h, c0b:c0b + BS], self.ident_bf[:, :])
                    kt = self.ktb[:, self.tmp("ktb", 4), :]
                    self.cp("act", kt[0:BS, :], self.psb[0:BS, o2:o2 + 128])
                    kts.append(kt)
                for _ in range(cpg):
                    j = gcnt[0]
                    gcnt[0] += 1
                    pg = self.proj_fm(hgw[j // 4], j % 4, T)
                    self.act(sg[:, j, :T], pg[:, :T], AF.Silu)
                    if j % 4 == 3:
                        self.w_release(hgi[j // 4])
                po = self.bank(pin=True)
                for ci in range(2):
                    r0 = ci * CL
                    ch = b * 2 + ci
                    if smp:
                        self.dma("sp", Sb[:, hh:hh + 4, :], self.state_hg[jj, ci, hh:hh + 4].rearrange("h d e -> d h e"))
                    for i4, h in enumerate(range(hh, hh + 4)):
                        sbs = Shb[:, h, 0:128]
                        self.act(sbs, Sb[:, h, :], AF.Copy, scale=eref[:, h, ch:ch + 1])
                        oc = i4 * 128 + r0
                        self.mm(po[:, oc:oc + CL], vt[r0:r0 + CL, b, h * 128:(h + 1) * 128],
                                Ps[i4][r0:r0 + CL, r0:r0 + CL], start=True, stop=False)
                        self.mm(po[:, oc:oc + CL], sbs, qt[:, h, c0b + r0:c0b + r0 + CL], start=False, stop=True)
                        pd = self.bank()
                        self.mm(pd[:, 0:128], kts[i4][r0:r0 + CL, :], vt[r0:r0 + CL, b, h * 128:(h + 1) * 128])
                        t1 = self.pd32[:, self.tmp("pd32", 2), :]
                        self.ts("dve", t1, pd[:, 0:128], ebr[:, h, ch:ch + 1], None, ALU.mult)
                        self.stt("dve", Sb[:, h, :], Sb[:, h, :], ebl[:, h, ch:ch + 1], t1, ALU.mult, ALU.add)
                    if smp:
                        self.outs.append(self.dma("sp", self.o_hg_s[jj, ci, hh:hh + 4].rearrange("h d e -> d h e"),
                                                  Sb[:, hh:hh + 4, :]))
                self.cp("dve", oT[:, hh:hh + 4, c0b:c0b + BS],
                        po[:, 0:512].rearrange("p (c n) -> p c n", c=4)[:, :, 0:BS])
                self.unpin(po)
        if (not smp) and last:
            self.outs.append(self.dma("sp", self.o_hg_p[jj].rearrange("h d e -> d h e"), Sb[:, :, :]))
        osq = self.sq
        zb = self.sg
        for h in range(8):
            self.act(osq[:, h, :T], oT[:, h, :T], AF.Square)
        for h in range(8):
            r = self.sumsq_rstd([osq[:, h, :T]], T, 128 * EPS)
            self.tt("pool", oT[:, h, :T], oT[:, h, :T], r, ALU.mult)
            gcol = 224 + h
            self.stt("dve", zb[:, h, :T], oT[:, h, :T], self.pvs[:, gcol:gcol + 1], sg[:, h, :T], ALU.mult, ALU.mult)
        self.w_out_proj(2, 8, 4, T)
        self.postnorm((li * 6 + 3) * 8, T)

    def load_x_chunk(self, t0, T, smp, c):
        xsrc = self.xsT if smp else self.xT
        c0 = 0 if smp else t0
        self.dma("sp", self.x[:, c, :T], xsrc[c * 128:(c + 1) * 128, c0:c0 + T])

    def load_aux(self, t0, T, smp):
        r0 = self.SEQ if smp else t0
        self.dma("sp", self.rope[:, :, :T], self.ropeT[:, :, r0:r0 + T])
        if smp:
            self.dma("sp", self.geo[:, :], self.geod[1])

    def tile(self, t0, T, smp, last, nxt, preloaded):
        self.sq_valid = False
        c0 = 0 if smp else t0
        if not preloaded:
            for c in range(8):
                self.load_x_chunk(t0, T, smp, c)
            self.load_aux(t0, T, smp)
        ydst = self.ysT if smp else self.yT

        def fin_chunk(c):
            self.outs.append(self.dma("sp", ydst[c * 128:(c + 1) * 128, c0:c0 + T], self.x[:, c, :T]))
            if nxt is not None:
                if c == 0:
                    self.load_aux(*nxt)
                self.load_x_chunk(nxt[0], nxt[1], nxt[2], c)

        self.xn_valid = False
        for li in range(self.depth):
            self.next_pre = (li * 6 + 2) * 8
            self.ffn(li, 0, T)
            kind, jj = li % 3, li // 3
            self.next_pre = (li * 6 + 4) * 8
            if kind == 0:
                self.retention(li, jj, T, smp, t0, last)
            elif kind == 1:
                self.swa(li, jj, T, smp, t0, last)
            else:
                self.hgrn(li, jj, T, smp, t0, last)
            self.next_pre = ((li + 1) * 6) * 8 if li + 1 < self.depth else None
            self.ffn(li, 1, T, after_chunk=(fin_chunk if li == self.depth - 1 else None))

    def build(self):
        nc = bass.Bass("TRN2", target_bir_lowering=False)
        self.nc = nc
        SEQ, T = self.SEQ, self.T
        nr, ns, nh = max(self.n_ret, 1), max(self.n_swa, 1), max(self.n_hg, 1)
        TOT = sum(n for _, n in self.units)
        GW = self.goff["GW"]

        def din(name, shape):
            return nc.dram_tensor(name, list(shape), F32, kind="ExternalInput").ap()

        def dout(name, shape):
            return nc.dram_tensor(name, list(shape), F32, kind="ExternalOutput").ap()

        self.xT = din("xT", [D, SEQ])
        self.xsT = din("xsT", [D, 64])
        self.wstream = din("wstream", [128, TOT])
        self.wscr = nc.dram_tensor("wscr", [128, TOT], BF16, kind="Internal").ap()
        self.cstd = din("cst", [128, self.ncst])
        self.geod = din("geo", [2, 128, GW])
        self.pvd = din("pv", [128, 264])
        self.ropeT = din("ropeT", [128, 2, SEQ + 64])
        self.sinkd = din("sinkpp", [128, 8])
        self.state_ret = din("state_ret", [nr, 2, 8, 128, 256])
        self.state_hg = din("state_hg", [nh, 2, 8, 128, 128])
        self.cache_k = din("cache_k", [ns, 2, 128, 2, 64])
        self.cache_v = din("cache_v", [ns, 2, 128, 2, 64])
        self.yT = dout("yT", [D, SEQ])
        self.ysT = dout("ysT", [D, 64])
        self.o_ret_p = dout("o_ret_p", [nr, 8, 128, 256])
        self.o_ret_s = dout("o_ret_s", [nr, 2, 8, 128, 256])
        self.o_k_p = dout("o_k_p", [ns, 2, 64, 128])
        self.o_v_p = dout("o_v_p", [ns, 128, 128])
        self.o_k_s = dout("o_k_s", [ns, 2, 64, 64])
        self.o_v_s = dout("o_v_s", [ns, 64, 128])
        self.o_hg_p = dout("o_hg_p", [nh, 8, 128, 128])
        self.o_hg_s = dout("o_hg_s", [nh, 2, 8, 128, 128])

        import contextlib
        with contextlib.ExitStack() as es:
            def sb(name, shape, dt):
                return es.enter_context(nc.sbuf_tensor(name, list(shape), dt))

            def pst(name, shape, dt):
                return es.enter_context(nc.psum_tensor(name, list(shape), dt))

            TW = max(T, 256)
            self.cst = sb("cst_sb", [128, self.ncst], F32)
            self.geo = sb("geo_sb", [128, GW], F32)
            self.pv = sb("pv_sb", [128, 264], F32)
            self.pvs = sb("pvs", [128, 232], F32)
            self.ident_bf = sb("ident_bf", [128, 128], BF16)
            self.ones_bf = sb("ones_bf", [128, 128], BF16)
            self.ones_lo_bf = sb("ones_lo_bf", [128, 128], BF16)
            self.ones_hi_bf = sb("ones_hi_bf", [128, 128], BF16)
            self.rrot_bf = sb("rrot_bf", [128, 128], BF16)
            self.lbe = sb("lbe", [128, 4, 8], F32)
            self.lb = sb("lb", [128, 8], F32)
            self.oml = sb("oml", [128, 8], F32)
            self.noml = sb("noml", [128, 8], F32)
            self.lbt = sb("lbt", [128, 2, 8], F32)
            self.es = sb("es", [128, 8], F32)
            self.x = sb("x", [128, 8, T], F32)
            self.xn = sb("xn", [128, 8, T], BF16)
            self.rstd = sb("rstd", [128, 2, T], F32)
            self.rope = sb("rope", [128, 2, T], F32)
            self.oT = sb("oT", [128, 8, T], F32)
            self.yb = self.oT
            self.sq = sb("sq", [128, 8, T], BF16)
            self.ftmp = sb("ftmp", [128, 4, TW], F32)
            self.qb = sb("qb", [128, 3, T], BF16)
            self.qr = sb("qr", [128, 8, T], BF16)
            self.kr = sb("kr", [128, 8, T], BF16)
            self.qc = sb("qc", [128, 4, T], BF16)
            self.sg = sb("sg", [128, 16, T], BF16)
            self.vt = sb("vt", [128, max(T // 128, 1), 1024], BF16)
            self.pbuf = sb("pbuf", [128, 4, 128], BF16)
            self.ktb = sb("ktb", [128, 4, 128], BF16)
            self.ptb = sb("ptb", [128, 4, 512], BF16)
            self.kl = sb("kl", [128, 2, 128 + T], BF16)
            self.kh = sb("kh", [128, 2, 128 + T], BF16)
            self.vlo = sb("vlo", [128, T // 128 + 1, 2, 128], BF16)
            self.vhi = sb("vhi", [128, T // 128 + 1, 2, 128], BF16)
            self.kout = sb("kout", [128, 2, 128], F32)
            self.vout = sb("vout", [128, 128], F32)
            self.kc32 = sb("kc32", [128, 2, 128], F32)
            self.pd32 = sb("pd32", [128, 2, 128], F32)
            self.dmy = sb("dmy", [128, 4], F32)
            self.ssx = sb("ssx", [128, T], F32)
            self.hsm = sb("hsm", [128, 2, 16], F32)
            self.ebl = sb("ebl", [128, 8, 16], F32)
            self.ebr = sb("ebr", [128, 8, 16], F32)
            self.eref = sb("eref", [128, 8, 16], F32)
            self.Sret = [sb("Sret%d" % i, [128, 8, 256], F32) for i in range(nr)]
            self.Sbf = sb("Sbf", [128, 8, 256], BF16)
            self.Shg = sb("Shg", [128, 8, 128], F32)
            self.wsl = [sb("wsl%d" % i, [128, SLOT], BF16) for i in range(NSLOT)]
            self.ps = [pst("ps%d" % i, [128, 512], F32) for i in range(7)]
            self.psb = pst("psb", [128, 1024], BF16)
            sems = {e: es.enter_context(nc.semaphore("sem_" + e)) for e in ("pe", "act", "dve", "pool", "sp")}
            dmasems = {q: [es.enter_context(nc.semaphore("dsem_%s%d" % (q, i))) for i in range(DMA_K)]
                       for q in ("sp", "pool")}
            self.outs = []
            if _os.environ.get("K_MARK"):
                print("SBUF bytes remaining", nc.sbuf_bytes_remaining)
            self.prologue()
            ntiles = SEQ // T
            self.w_init(ntiles + (1 if self.do_sample else 0))
            tl = [(ti * T, T, False) for ti in range(ntiles)]
            if self.do_sample:
                tl.append((0, 64, True))
            for k, (t0_, T_, smp_) in enumerate(tl):
                nxt = tl[k + 1] if k + 1 < len(tl) else None
                self.tile(t0_, T_, smp_, smp_ or (t0_ + T_ == SEQ), nxt, k > 0)
            import os
            mx = int(os.environ.get("K_MAXOPS", "0"))
            if _DBG:
                lo, hi = [int(v) for v in os.environ["K_DUMP"].split(":")]
                for i in range(lo, min(hi, len(self.S.ops))):
                    print(i, self.S.ops[i].dbg)
            if mx > 0:
                print("TOTAL OPS", len(self.S.ops), "truncating to", mx)
                self.S.ops = self.S.ops[:mx]
                self.outs = [o for o in self.outs if o in set(self.S.ops)]
            fin = self.S.add("sp", None, [], [])
            for o in self.outs:
                fin.deps.add(o)
            self.S.finalize(nc, sems, dmasems)
            with nc.Block() as block:
                self.S.emit(block)
        return nc

    def prologue(self):
        co = self.coff
        cst = self.cst
        self.dma("sp", cst[:, :], self.cstd[:, :])
        self.dma("sp", self.geo[:, :], self.geod[0])
        self.dma("sp", self.pv[:, :], self.pvd[:, :])
        self.dma("sp", self.es[:, :], self.sinkd[:, :])
        for name, dst in (("ident", self.ident_bf), ("ones", self.ones_bf), ("ones_lo", self.ones_lo_bf),
                          ("ones_hi", self.ones_hi_bf), ("rrot", self.rrot_bf)):
            self.cp("dve", dst[:, :], cst[:, co[name]:co[name] + 128])
        self.tt("dve", self.pvs[:, :], self.pv[:, 0:232], cst[:, co["gmul"]:co["gmul"] + 232], ALU.mult)
        self.act(self.es[:, :], self.es[:, :], AF.Exp)
        lbe = self.lbe
        self.act(lbe[:, :, :], self.pv[:, 232:264].rearrange("p (l c) -> p l c", l=4), AF.Exp)
        t = self.lbt
        self.tt("dve", t[:, 0, :], lbe[:, 1, :], lbe[:, 2, :], ALU.add)
        self.tt("dve", t[:, 1, :], lbe[:, 0, :], lbe[:, 3, :], ALU.add)
        self.tt("dve", t[:, 1, :], t[:, 1, :], t[:, 0, :], ALU.add)
        self.S.add("dve", lambda e: e.reciprocal(t[:, 1, :], t[:, 1, :]), [t[:, 1, :]], [t[:, 1, :]])
        self.tt("dve", self.lb[:, :], t[:, 0, :], t[:, 1, :], ALU.mult)
        self.ts("dve", self.oml[:, :], self.lb[:, :], -1.0, 1.0, ALU.mult, ALU.add)
        self.ts("dve", self.noml[:, :], self.oml[:, :], -1.0, None, ALU.mult)
        for S_ in self.Sret:
            self.memset("pool", S_[:, :, :], 0.0)
        self.memset("pool", self.Shg[:, :, :], 0.0)
        self.memset("pool", self.vlo[:, :, :, :], 0.0)
        self.memset("pool", self.vhi[:, :, :, :], 0.0)
        self.memset("pool", self.kl[:, :, :], 0.0)
        self.memset("pool", self.kh[:, :, :], 0.0)


_CACHE = {}


def prepare_static(SEQ, T, past):
    cb, gam, geo, goff = make_consts(T)
    return cb.build(), cb.off, gam, geo, goff, make_rope(SEQ, past)


def run(inputs, SEQ, T, depth, n_cores, do_sample=True, past=4096, core_ids=None):
    inp = {k: np.asarray(v) for k, v in inputs.items()}
    cst, coff, gam, geo, goff, rope = prepare_static(SEQ, T, past)
    wstream, units = build_wstream(inp, depth)
    n_ret, n_swa, n_hg = (depth + 2) // 3, (depth + 1) // 3, depth // 3
    nr, ns, nh = max(n_ret, 1), max(n_swa, 1), max(n_hg, 1)
    pv = np.zeros((128, 264), np.float32)
    pv[:, 0:192] = inp["norm_g"].reshape(4, 6, 8, 128).transpose(3, 0, 1, 2).reshape(128, 192)
    pv[:, 192:224] = inp["ret_gn_g"].reshape(2, 16, 128).transpose(2, 0, 1).reshape(128, 32)
    pv[:, 224:232] = inp["hg_gn_g"].reshape(1, 8, 128)[0].T
    pv[:, 232:264] = inp["hg_lb"].reshape(4, 8, 128).transpose(2, 0, 1).reshape(128, 32)
    sink = inp["swa_sink"][0]
    sinkpp = np.zeros((128, 8), np.float32)
    for hp in range(8):
        sinkpp[:64, hp] = sink[2 * hp]
        sinkpp[64:, hp] = sink[2 * hp + 1]
    prog = Prog(SEQ, T, depth, units, cst.shape[1], coff, goff, gam, do_sample=do_sample, past=past)
    nc = prog.build()
    in_maps = []
    for c in range(n_cores):
        m = {
            "xT": np.ascontiguousarray(inp["x_prompt"][c, :SEQ].T),
            "xsT": np.ascontiguousarray(inp["x_sample"][2 * c:2 * c + 2].reshape(64, D).T),
            "wstream": wstream, "cst": cst, "geo": geo, "pv": pv, "ropeT": rope, "sinkpp": sinkpp,
            "state_ret": np.ascontiguousarray(inp["state_ret"][:nr, 2 * c:2 * c + 2]),
            "state_hg": np.ascontiguousarray(inp["state_hgrn"][:nh, 2 * c:2 * c + 2]),
            "cache_k": np.ascontiguousarray(inp["cache_swa_k"][:ns, 2 * c:2 * c + 2]),
            "cache_v": np.ascontiguousarray(inp["cache_swa_v"][:ns, 2 * c:2 * c + 2]),
        }
        in_maps.append(m)
    ids = list(range(n_cores)) if core_ids is None else core_ids
    res = run_bass_kernel_spmd(nc, in_maps, core_ids=ids)
    return res.results, (n_ret, n_swa, n_hg)


def assemble(results, counts, SEQ):
    n_ret, n_swa, n_hg = counts
    nco = len(results)
    y_p = np.stack([r["yT"].T for r in results], 0)
    y_s = np.concatenate([r["ysT"].T.reshape(2, 32, D) for r in results], 0)
    ret_p = np.stack([r["o_ret_p"][:n_ret] for r in results], 1)
    ret_s = np.concatenate([r["o_ret_s"][:n_ret] for r in results], 1)
    k_p = np.stack([r["o_k_p"][:n_swa].transpose(0, 3, 1, 2) for r in results], 1)
    v_p = np.stack([r["o_v_p"][:n_swa].reshape(n_swa, 128, 2, 64) for r in results], 1)
    k_s = np.concatenate([r["o_k_s"][:n_swa].transpose(0, 3, 1, 2).reshape(n_swa, 2, 32, 2, 64) for r in results], 1)
    v_s = np.concatenate([r["o_v_s"][:n_swa].reshape(n_swa, 2, 32, 2, 64) for r in results], 1)
    hg_p = np.stack([r["o_hg_p"][:n_hg] for r in results], 1)
    hg_s = np.concatenate([r["o_hg_s"][:n_hg] for r in results], 1)
    f = lambda a: np.ascontiguousarray(a, dtype=np.float32)
    return tuple(f(a) for a in (y_p, y_s, ret_p, ret_s, k_p, v_p, k_s, v_s, hg_p, hg_s))


def kernel(**inputs):
    results, counts = run(inputs, SEQ=4096, T=512, depth=4, n_cores=8)
    return assemble(results, counts, 4096)
```
